# Optimizing a Trainium2 kernel written in Bass

```python
import math
import jax, jax.numpy as jnp
from jax import lax
import numpy as np

D_MODEL = 1024
BATCH = 2
SEQ = 8192
DEPTH = 2
DEC_BATCH = 32
DEC_SEQ = 2048
PAST_LEN = 128

HEAD_DIM = 64
N_HEADS_A = 8
N_HEADS_B = 8
N_HEADS_C = 8
N_HEADS_D = 8
WIDTH_A = N_HEADS_A * HEAD_DIM
WIDTH_B = N_HEADS_B * HEAD_DIM
WIDTH_C = N_HEADS_C * HEAD_DIM
WIDTH_D = N_HEADS_D * HEAD_DIM
DILATED_BRANCHES = ((128, 1), (512, 4), (2048, 16))
RWKV_DECAY_RANK = 64
RWKV_ICL_RANK = 64
RWKV_GATE_RANK = 128
SSD_STATE = 128
SSD_GROUPS = 2
SSD_CONV = 5
SSD_XBC = WIDTH_D + 2 * SSD_GROUPS * SSD_STATE
CHUNK = 128
D_FF = 4 * D_MODEL
N_EVEN = (DEPTH + 1) // 2
N_ODD = DEPTH // 2
RWKV_COLS = 3 * WIDTH_B + 2 * RWKV_DECAY_RANK + 2 * RWKV_ICL_RANK + RWKV_GATE_RANK
EVEN_IN = 3 * WIDTH_A + RWKV_COLS
ODD_IN = 4 * WIDTH_C + WIDTH_D + SSD_XBC + 2 * N_HEADS_D
MIX_OUT_EVEN = WIDTH_A + WIDTH_B
MIX_OUT_ODD = WIDTH_C + WIDTH_D
NORM_EPS = 1e-6
GN_EPS = 1e-5
RWKV_GN_EPS = 64e-5
ROPE_BASE = 10000.0

kernel_name = 'hybrid_bidir_encoder_dilated_rwkv7_retnet_ssd'


def split_cols(u, sizes):
    return jnp.split(u, [int(c) for c in np.cumsum(sizes)[:-1]], axis=-1)


def rms_norm(x, g):
    xf = x.astype(jnp.float32)
    return xf * lax.rsqrt(jnp.mean(xf * xf, -1, keepdims=True) + NORM_EPS) * g


def head_norm(y, gain, bias, eps):
    mu = jnp.mean(y, -1, keepdims=True)
    var = jnp.mean(jnp.square(y - mu), -1, keepdims=True)
    yn = (y - mu) * lax.rsqrt(var + eps)
    return yn.reshape(*y.shape[:-2], -1) * gain + bias


def alibi_slopes(n_heads):
    return jnp.exp2(-8.0 * (jnp.arange(n_heads, dtype=jnp.float32) + 1.0) / n_heads)


def rotary(u):
    t = u.shape[1]
    half = HEAD_DIM // 2
    inv = ROPE_BASE ** (-jnp.arange(half, dtype=jnp.float32) / half)
    ang = jnp.arange(t, dtype=jnp.float32)[:, None] * inv[None, :]
    cos = jnp.cos(ang)[None, :, None, :]
    sin = jnp.sin(ang)[None, :, None, :]
    u1, u2 = u[..., :half], u[..., half:]
    return jnp.concatenate([u1 * cos - u2 * sin, u1 * sin + u2 * cos], -1)


def centred_shift(u, mu_prev, mu_next):
    prev = jnp.pad(u, ((0, 0), (1, 0), (0, 0)))[:, :-1]
    nxt = jnp.pad(u, ((0, 0), (0, 1), (0, 0)))[:, 1:]
    return u + mu_prev * (prev - u) + mu_next * (nxt - u)


def centred_dwconv(u, w, bias):
    c = u.shape[-1]
    pad = w.shape[0] // 2
    out = lax.conv_general_dilated(u, w[:, None, :].astype(u.dtype), window_strides=(1,),
                                   padding=[(pad, pad)], dimension_numbers=('NWC', 'WIO', 'NWC'),
                                   feature_group_count=c)
    return out + bias


def dilated_branch(q, k, v, slopes, window, dilation):
    b, t, h, dh = q.shape
    r = window // (2 * dilation)
    L = t // dilation
    nb = -(-L // r)
    lp = nb * r

    def to_sub(u):
        return u.reshape(b, L, dilation, h, dh).transpose(0, 2, 1, 3, 4)

    qs = jnp.pad(to_sub(q), ((0, 0), (0, 0), (0, lp - L), (0, 0), (0, 0))).reshape(b, dilation, nb, r, h, dh)
    kpad = ((0, 0), (0, 0), (r, lp - L + r), (0, 0), (0, 0))
    ks = jnp.pad(to_sub(k), kpad).reshape(b, dilation, nb + 2, r, h, dh)
    vs = jnp.pad(to_sub(v), kpad).reshape(b, dilation, nb + 2, r, h, dh)
    kb = jnp.concatenate([ks[:, :, :-2], ks[:, :, 1:-1], ks[:, :, 2:]], axis=3)
    vb = jnp.concatenate([vs[:, :, :-2], vs[:, :, 1:-1], vs[:, :, 2:]], axis=3)
    s = jnp.einsum('bdnqhe,bdnkhe->bdnhqk', qs, kb)
    qpos = jnp.arange(nb)[:, None] * r + jnp.arange(r)[None, :]
    kpos = jnp.arange(nb)[:, None] * r - r + jnp.arange(3 * r)[None, :]
    rel = kpos[:, None, :] - qpos[:, :, None]
    valid = (jnp.abs(rel) <= r) & (kpos[:, None, :] >= 0) & (kpos[:, None, :] < L)
    dist = (jnp.abs(rel) * dilation).astype(jnp.float32)
    alibi = -slopes[None, :, None, None] * dist[:, None]
    s = jnp.where(valid[None, None, :, None], s + alibi[None, None], -jnp.inf)
    m = jnp.max(s, -1, keepdims=True)
    p = jnp.exp(s - m)
    den = jnp.sum(p, -1, keepdims=True)
    o = jnp.einsum('bdnhqk,bdnkhe->bdnqhe', p / den, vb)
    lse = (m + jnp.log(den))[..., 0]
    o = o.reshape(b, dilation, lp, h, dh)[:, :, :L].transpose(0, 2, 1, 3, 4).reshape(b, t, h, dh)
    lse = lse.transpose(0, 1, 2, 4, 3).reshape(b, dilation, lp, h)[:, :, :L]
    lse = lse.transpose(0, 2, 1, 3).reshape(b, t, h)
    return o, lse


def dilated_attention(q, k, v, q_gain, k_gain):
    b, t, _ = q.shape
    hs = lambda z: z.reshape(b, t, N_HEADS_A, HEAD_DIM)
    qh = rms_norm(hs(q), q_gain) * (HEAD_DIM ** -0.5)
    kh = rms_norm(hs(k), k_gain)
    vh = hs(v)
    slopes = alibi_slopes(N_HEADS_A)
    outs, lses = [], []
    for window, dilation in DILATED_BRANCHES:
        o, lse = dilated_branch(qh, kh, vh, slopes, window, dilation)
        outs.append(o)
        lses.append(lse)
    wts = jax.nn.softmax(jnp.stack(lses, 0), axis=0)
    o = jnp.einsum('nbth,nbthe->bthe', wts, jnp.stack(outs, 0))
    return o.reshape(b, t, WIDTH_A)


def rwkv7_scan(r, w, k, v, kk, a):
    b, t, h, n = r.shape

    def step(S, inp):
        r_t, w_t, k_t, v_t, kk_t, a_t = inp
        sa = jnp.einsum('bhvk,bhk->bhv', S, -kk_t)
        S = S * w_t[:, :, None, :] + sa[..., None] * (kk_t * a_t)[:, :, None, :] + v_t[..., None] * k_t[:, :, None, :]
        return S, jnp.einsum('bhvk,bhk->bhv', S, r_t)

    xs = tuple(u.transpose(1, 0, 2, 3) for u in (r, w, k, v, kk, a))
    _, y = lax.scan(step, jnp.zeros((b, h, n, n), jnp.float32), xs)
    return y.transpose(1, 0, 2, 3)


def rwkv7_mixer(cols, mu_prev, mu_next, w0, w2, a0, a2, g2, k_k, k_a, r_k, ln_g, ln_b):
    b, t, _ = cols.shape
    cols = centred_shift(cols, mu_prev, mu_next)
    r, k, v, zw, za, zg = split_cols(cols, (WIDTH_B, WIDTH_B, WIDTH_B, 2 * RWKV_DECAY_RANK, 2 * RWKV_ICL_RANK, RWKV_GATE_RANK))
    hs = lambda z: z.reshape(b, t, N_HEADS_B, HEAD_DIM)
    kk = hs(k * k_k)
    kk = kk / jnp.maximum(jnp.sqrt(jnp.sum(kk * kk, -1, keepdims=True)), 1e-12)
    g = jnp.einsum('btr,re->bte', jax.nn.sigmoid(zg), g2)
    zw = zw.reshape(b, t, 2, RWKV_DECAY_RANK)
    za = za.reshape(b, t, 2, RWKV_ICL_RANK)
    flip = lambda z: jnp.flip(z, 1)
    ys = []
    for d in range(2):
        w_log = -jax.nn.softplus(-(w0[d] + jnp.einsum('btr,re->bte', jnp.tanh(zw[:, :, d]), w2[d]))) - 0.5
        decay = jnp.exp(-jnp.exp(w_log))
        a = jax.nn.sigmoid(a0[d] + jnp.einsum('btr,re->bte', za[:, :, d], a2[d]))
        k_rep = k * (1.0 + (a - 1.0) * k_a)
        args = (hs(r), hs(decay), hs(k_rep), hs(v), kk, hs(a))
        if d == 0:
            ys.append(rwkv7_scan(*args))
        else:
            ys.append(flip(rwkv7_scan(*[flip(z) for z in args])))
    y = head_norm(ys[0] + ys[1], ln_g, ln_b, RWKV_GN_EPS)
    bonus = jnp.sum(hs(r) * hs(k) * r_k, -1, keepdims=True) * hs(v)
    return (y + bonus.reshape(b, t, WIDTH_B)) * g


def chunked_decay_attention(q, k, v, log_a):
    b, t, h, dk = q.shape
    dv = v.shape[-1]
    nc = t // CHUNK
    qc = q.reshape(b, nc, CHUNK, h, dk)
    kc = k.reshape(b, nc, CHUNK, h, dk)
    vc = v.reshape(b, nc, CHUNK, h, dv)
    cum = jnp.cumsum(log_a.reshape(b, nc, CHUNK, h), axis=2)
    causal = jnp.tril(jnp.ones((CHUNK, CHUNK), bool))
    seg = cum.transpose(0, 1, 3, 2)
    diff = seg[..., :, None] - seg[..., None, :]
    decay = jnp.where(causal, jnp.exp(jnp.where(causal, diff, 0.0)), 0.0)
    s = jnp.einsum('bclhk,bcshk->bchls', qc, kc) * decay
    y_intra = jnp.einsum('bchls,bcshv->bclhv', s, vc)
    w_end = jnp.exp(cum[:, :, -1:] - cum)
    states = jnp.einsum('bcshk,bcsh,bcshv->bchkv', kc, w_end, vc)
    total = jnp.exp(cum[:, :, -1])

    def step(hprev, inp):
        st, tot = inp
        return hprev * tot[..., None, None] + st, hprev

    _, hprevs = lax.scan(step, jnp.zeros((b, h, dk, dv), jnp.float32),
                         (states.transpose(1, 0, 2, 3, 4), total.transpose(1, 0, 2)))
    hprevs = hprevs.transpose(1, 0, 2, 3, 4)
    y_inter = jnp.einsum('bclhk,bclh,bchkv->bclhv', qc, jnp.exp(cum), hprevs)
    return (y_intra + y_inter).reshape(b, t, h, dv)


def bidirectional_decay_attention(q, k, v_f, la_f, v_b, la_b):
    flip = lambda z: jnp.flip(z, 1)
    y_f = chunked_decay_attention(q, k, v_f, la_f)
    y_b = flip(chunked_decay_attention(flip(q), flip(k), flip(v_b), flip(la_b)))
    return y_f + y_b


def retention_mixer(q, k, v, g, decay_exp, gn_g, gn_b):
    b, t, _ = q.shape
    hs = lambda z: z.reshape(b, t, N_HEADS_C, HEAD_DIM)
    qh = rotary(hs(q))
    kh = rotary(hs(k)) * (HEAD_DIM ** -0.5)
    vh = hs(v)
    log_gamma = jnp.log1p(-jnp.exp2(-decay_exp.astype(jnp.float32)))
    la_f = jnp.broadcast_to(log_gamma[0], (b, t, N_HEADS_C))
    la_b = jnp.broadcast_to(log_gamma[1], (b, t, N_HEADS_C))
    y = bidirectional_decay_attention(qh, kh, vh, la_f, vh, la_b)
    return head_norm(y, gn_g, gn_b, GN_EPS) * jax.nn.silu(g)


def ssd_mixer(z, xbc, dt, conv_w, conv_b, dt_bias, a_log, d_skip, norm_g):
    b, t, _ = z.shape
    xbc = jax.nn.silu(centred_dwconv(xbc, conv_w, conv_b))
    xs, bm, cm = split_cols(xbc, (WIDTH_D, SSD_GROUPS * SSD_STATE, SSD_GROUPS * SSD_STATE))
    rep = N_HEADS_D // SSD_GROUPS
    xh = xs.reshape(b, t, N_HEADS_D, HEAD_DIM)
    bh = jnp.repeat(bm.reshape(b, t, SSD_GROUPS, SSD_STATE), rep, axis=2)
    ch = jnp.repeat(cm.reshape(b, t, SSD_GROUPS, SSD_STATE), rep, axis=2)
    dts = jax.nn.softplus(dt.reshape(b, t, 2, N_HEADS_D) + dt_bias)
    la = dts * (-jnp.exp(a_log.astype(jnp.float32)))
    y = bidirectional_decay_attention(ch, bh, xh * dts[:, :, 0, :, None], la[:, :, 0],
                                      xh * dts[:, :, 1, :, None], la[:, :, 1])
    y = (y + d_skip[:, None] * xh).reshape(b, t, WIDTH_D) * jax.nn.silu(z)
    yg = y.reshape(b, t, SSD_GROUPS, -1)
    yg = yg * lax.rsqrt(jnp.mean(yg * yg, -1, keepdims=True) + NORM_EPS)
    return yg.reshape(b, t, WIDTH_D) * norm_g


def even_mixer(u, p, i):
    proj = jnp.einsum('btd,de->bte', u, p['even_in_w'][i]).astype(jnp.float32)
    qa, ka, va, rcols = split_cols(proj, (WIDTH_A, WIDTH_A, WIDTH_A, RWKV_COLS))
    y_a = dilated_attention(qa, ka, va, p['attn_q_gain'][i], p['attn_k_gain'][i])
    y_b = rwkv7_mixer(rcols, p['rwkv_mu_prev'][i], p['rwkv_mu_next'][i], p['rwkv_w0'][i], p['rwkv_w2'][i],
                      p['rwkv_a0'][i], p['rwkv_a2'][i], p['rwkv_g2'][i], p['rwkv_k_k'][i], p['rwkv_k_a'][i],
                      p['rwkv_r_k'][i], p['rwkv_ln_g'][i], p['rwkv_ln_b'][i])
    y = jnp.concatenate([y_a, y_b], -1)
    return jnp.einsum('bte,ed->btd', y, p['even_out_w'][i])


def odd_mixer(u, p, i):
    proj = jnp.einsum('btd,de->bte', u, p['odd_in_w'][i]).astype(jnp.float32)
    qc, kc, vc, gc, z, xbc, dt = split_cols(proj, (WIDTH_C, WIDTH_C, WIDTH_C, WIDTH_C, WIDTH_D, SSD_XBC, 2 * N_HEADS_D))
    y_c = retention_mixer(qc, kc, vc, gc, p['ret_decay_exp'][i], p['ret_gn_g'][i], p['ret_gn_b'][i])
    y_d = ssd_mixer(z, xbc, dt, p['ssd_conv_w'][i], p['ssd_conv_b'][i], p['ssd_dt_bias'][i],
                    p['ssd_a_log'][i], p['ssd_d'][i], p['ssd_norm_g'][i])
    y = jnp.concatenate([y_c, y_d], -1)
    return jnp.einsum('bte,ed->btd', y, p['odd_out_w'][i])


def squared_relu_mlp(u, w1, w2):
    hdn = jnp.square(jax.nn.relu(jnp.einsum('btd,df->btf', u, w1)))
    return jnp.einsum('btf,fd->btd', hdn, w2)


def trunk(x, p):
    h = x
    for layer in range(DEPTH):
        u = rms_norm(h, p['norm_mix'][layer])
        i = layer // 2
        if layer % 2 == 0:
            mix = even_mixer(u, p, i)
        else:
            mix = odd_mixer(u, p, i)
        h = h + mix
        u = rms_norm(h, p['norm_mlp'][layer])
        h = h + squared_relu_mlp(u, p['mlp_w1'][layer], p['mlp_w2'][layer])
    return h.astype(x.dtype)


def setup_inputs(seed: int = 0) -> dict:
    key = jax.random.key(seed)
    ks = jax.random.split(key, 48)
    it = iter(range(48))
    f32 = jnp.float32

    def nrm(shape, scale):
        return scale * jax.random.normal(ks[next(it)], shape, f32)

    def gain(shape):
        return 1.0 + 0.02 * jax.random.normal(ks[next(it)], shape, f32)

    def unif(shape, lo, hi):
        return jax.random.uniform(ks[next(it)], shape, f32, lo, hi)

    dt0 = jnp.exp(unif((N_ODD, 2, N_HEADS_D), math.log(1e-3), math.log(1e-1)))
    return {
        'x_prompt': nrm((BATCH, SEQ, D_MODEL), 1.0),
        'x_sample': nrm((DEC_BATCH, DEC_SEQ, D_MODEL), 1.0),
        'norm_mix': gain((DEPTH, D_MODEL)),
        'norm_mlp': gain((DEPTH, D_MODEL)),
        'mlp_w1': nrm((DEPTH, D_MODEL, D_FF), D_MODEL ** -0.5),
        'mlp_w2': nrm((DEPTH, D_FF, D_MODEL), 0.5 * D_FF ** -0.5),
        'even_in_w': nrm((N_EVEN, D_MODEL, EVEN_IN), D_MODEL ** -0.5),
        'even_out_w': nrm((N_EVEN, MIX_OUT_EVEN, D_MODEL), 0.5 * MIX_OUT_EVEN ** -0.5),
        'attn_q_gain': gain((N_EVEN, HEAD_DIM)),
        'attn_k_gain': gain((N_EVEN, HEAD_DIM)),
        'rwkv_mu_prev': unif((N_EVEN, RWKV_COLS), 0.0, 0.5),
        'rwkv_mu_next': unif((N_EVEN, RWKV_COLS), 0.0, 0.5),
        'rwkv_w0': unif((N_EVEN, 2, WIDTH_B), -6.0, -1.0),
        'rwkv_w2': nrm((N_EVEN, 2, RWKV_DECAY_RANK, WIDTH_B), 0.1 * RWKV_DECAY_RANK ** -0.5),
        'rwkv_a0': nrm((N_EVEN, 2, WIDTH_B), 0.1),
        'rwkv_a2': nrm((N_EVEN, 2, RWKV_ICL_RANK, WIDTH_B), 0.5 * RWKV_ICL_RANK ** -0.5),
        'rwkv_g2': nrm((N_EVEN, RWKV_GATE_RANK, WIDTH_B), RWKV_GATE_RANK ** -0.5),
        'rwkv_k_k': 0.85 + nrm((N_EVEN, WIDTH_B), 0.05),
        'rwkv_k_a': 1.0 + nrm((N_EVEN, WIDTH_B), 0.05),
        'rwkv_r_k': nrm((N_EVEN, N_HEADS_B, HEAD_DIM), 0.1),
        'rwkv_ln_g': gain((N_EVEN, WIDTH_B)),
        'rwkv_ln_b': nrm((N_EVEN, WIDTH_B), 0.01),
        'odd_in_w': nrm((N_ODD, D_MODEL, ODD_IN), D_MODEL ** -0.5),
        'odd_out_w': nrm((N_ODD, MIX_OUT_ODD, D_MODEL), 0.5 * MIX_OUT_ODD ** -0.5),
        'ret_decay_exp': 5.0 + jnp.arange(N_HEADS_C, dtype=f32)[None, None, :] + nrm((N_ODD, 2, N_HEADS_C), 0.1),
        'ret_gn_g': gain((N_ODD, WIDTH_C)),
        'ret_gn_b': nrm((N_ODD, WIDTH_C), 0.01),
        'ssd_conv_w': nrm((N_ODD, SSD_CONV, SSD_XBC), SSD_CONV ** -0.5),
        'ssd_conv_b': nrm((N_ODD, SSD_XBC), 0.01),
        'ssd_dt_bias': dt0 + jnp.log(-jnp.expm1(-dt0)),
        'ssd_a_log': jnp.log(unif((N_ODD, 2, N_HEADS_D), 1.0, 16.0)),
        'ssd_d': 1.0 + nrm((N_ODD, N_HEADS_D), 0.1),
        'ssd_norm_g': gain((N_ODD, WIDTH_D)),
    }


def reference(x_prompt, x_sample, norm_mix, norm_mlp, mlp_w1, mlp_w2, even_in_w, even_out_w,
              attn_q_gain, attn_k_gain, rwkv_mu_prev, rwkv_mu_next, rwkv_w0, rwkv_w2, rwkv_a0, rwkv_a2,
              rwkv_g2, rwkv_k_k, rwkv_k_a, rwkv_r_k, rwkv_ln_g, rwkv_ln_b, odd_in_w, odd_out_w,
              ret_decay_exp, ret_gn_g, ret_gn_b, ssd_conv_w, ssd_conv_b, ssd_dt_bias, ssd_a_log, ssd_d,
              ssd_norm_g):
    p = dict(norm_mix=norm_mix, norm_mlp=norm_mlp, mlp_w1=mlp_w1, mlp_w2=mlp_w2,
             even_in_w=even_in_w, even_out_w=even_out_w, attn_q_gain=attn_q_gain, attn_k_gain=attn_k_gain,
             rwkv_mu_prev=rwkv_mu_prev, rwkv_mu_next=rwkv_mu_next, rwkv_w0=rwkv_w0, rwkv_w2=rwkv_w2,
             rwkv_a0=rwkv_a0, rwkv_a2=rwkv_a2, rwkv_g2=rwkv_g2, rwkv_k_k=rwkv_k_k, rwkv_k_a=rwkv_k_a,
             rwkv_r_k=rwkv_r_k, rwkv_ln_g=rwkv_ln_g, rwkv_ln_b=rwkv_ln_b,
             odd_in_w=odd_in_w, odd_out_w=odd_out_w, ret_decay_exp=ret_decay_exp, ret_gn_g=ret_gn_g,
             ret_gn_b=ret_gn_b, ssd_conv_w=ssd_conv_w, ssd_conv_b=ssd_conv_b, ssd_dt_bias=ssd_dt_bias,
             ssd_a_log=ssd_a_log, ssd_d=ssd_d, ssd_norm_g=ssd_norm_g)
    y_prompt = trunk(x_prompt, p)
    y_sample = trunk(x_sample, p)
    return (y_prompt, y_sample)
```

```python
import numpy as np
from contextlib import ExitStack
import concourse.bass as bass
import concourse.mybir as mybir

F32 = mybir.dt.float32
BF16 = mybir.dt.bfloat16
AF = mybir.ActivationFunctionType
ALU = mybir.AluOpType
AX = mybir.AxisListType
ENG = ['pe', 'act', 'dve', 'pool', 'sp']


class T:
    _n = 0

    def __init__(s, ap, name=''):
        s.ap = ap
        s.lw = None
        s.rd = {}
        s.cnt = 0
        T._n += 1
        s.id = T._n
        s.name = name

    def __getitem__(s, k):
        return V(s, s.ap[k])

    @property
    def v(s):
        return V(s, s.ap)

    @property
    def t(s):
        return s


class D(T):
    def __init__(s, ap, name=''):
        T.__init__(s, ap, name)
        s.writes = {}
        s.reads = {}


class V:
    def __init__(s, t, ap):
        s.t = t
        s.ap = ap

    def __getitem__(s, k):
        return V(s.t, s.ap[k])

    def r(s, pat, **kw):
        return V(s.t, s.ap.rearrange(pat, **kw))

    def bc(s, shape):
        return V(s.t, s.ap.to_broadcast(list(shape)))

    def bitcast(s, dt):
        return V(s.t, s.ap.bitcast(dt))

    @property
    def shape(s):
        return s.ap.shape

    @property
    def v(s):
        return s


def _ap(x):
    return x.ap if isinstance(x, V) else x


class Prog:
    def __init__(s, nc, es):
        s.nc = nc
        s.es = es
        s.ops = {e: [] for e in ENG}
        s.seq = {e: 0 for e in ENG}
        s.known = {e: {} for e in ENG}
        s.semh = {}
        for e in ENG:
            s.semh[('e', e)] = es.enter_context(nc.semaphore('sem_' + e))
        s.dma_tiles = []
        s.slots = []
        s.free_slots = {'sw': [], 'hw': []}
        s.nops = 0

    def _emit(s, eng, fn, deps, inc, seq=None):
        if eng == 'pe':
            deps.pop(('e', 'pe'), None)
        kn = s.known[eng]
        waits = []
        for k, v in deps.items():
            if v <= kn.get(k, 0):
                continue
            kn[k] = v
            waits.append((k, v))
        s.ops[eng].append((fn, waits, inc, seq))
        s.nops += 1

    @staticmethod
    def _add(deps, d):
        if d is None:
            return
        k, v = d
        if deps.get(k, 0) < v:
            deps[k] = v

    NO_POOL = False

    def op(s, eng, fn, reads=(), writes=()):
        if eng == 'pool' and Prog.NO_POOL:
            eng = 'dve'
        deps = {}
        for t in reads:
            s._add(deps, t.lw)
        for t in writes:
            s._add(deps, t.lw)
            for d in t.rd.items():
                s._add(deps, d)
        s.seq[eng] += 1
        me = (('e', eng), s.seq[eng])
        s._emit(eng, fn, deps, None, me[1])
        for t in reads:
            if t.rd.get(me[0], 0) < me[1]:
                t.rd[me[0]] = me[1]
        for t in writes:
            t.lw = me
            t.rd = {}

    def _dsem(s, t, eng='sp'):
        cls = 'sw' if eng == 'pool' else 'hw'
        if not hasattr(t, 'slot') or t.slot is None:
            t.slot = {}
        if cls not in t.slot:
            fl = s.free_slots[cls]
            if fl:
                idx = fl.pop()
            else:
                idx = len(s.slots)
                h = s.es.enter_context(s.nc.semaphore('dsem%d' % idx))
                s.slots.append([h, 0, cls])
                s.semh[('d', idx)] = h
            t.slot[cls] = idx
            s.dma_tiles.append(t)
        return ('d', t.slot[cls])

    def dma(s, eng, out, in_, **kw):
        load = isinstance(in_.t, D)
        sb = out.t if load else in_.t
        dr = in_.t if load else out.t
        deps = {}
        if load:
            s._add(deps, sb.lw)
            for d in sb.rd.items():
                s._add(deps, d)
            for d in dr.writes.items():
                s._add(deps, d)
        else:
            s._add(deps, sb.lw)
            for d in dr.writes.items():
                s._add(deps, d)
            for d in dr.reads.items():
                s._add(deps, d)
        k = s._dsem(sb, eng)
        s.slots[k[1]][1] += 16
        cnt = s.slots[k[1]][1]
        me = (k, cnt)
        oa, ia = out.ap, in_.ap
        s._emit(eng, lambda e: e.dma_start(out=oa, in_=ia, **kw), deps, k)
        if load:
            sb.lw = me
            sb.rd = {}
            dr.reads[k] = cnt
        else:
            sb.rd[k] = cnt
            dr.writes[k] = cnt

    def barrier(s):
        deps = {}
        for e in ENG:
            if s.seq[e] > 0:
                deps[('e', e)] = s.seq[e]
        for idx, (h, c, _c) in enumerate(s.slots):
            if c > 0:
                deps[('d', idx)] = c
        for e in ENG:
            kn = s.known[e]
            waits = []
            for k, v in deps.items():
                if k == ('e', e) and e == 'pe':
                    continue
                if v <= kn.get(k, 0):
                    continue
                kn[k] = v
                waits.append((k, v))
            if waits:
                s.ops[e].append((None, waits, None, None))
        for t in s.dma_tiles:
            t.slot = None
        s.dma_tiles = []
        s.free_slots = {'sw': [i for i, x in enumerate(s.slots) if x[2] == 'sw'],
                        'hw': [i for i, x in enumerate(s.slots) if x[2] == 'hw']}

    def emit(s):
        nc = s.nc
        needed = {e: set() for e in ENG}
        for e in ENG:
            for fn, waits, inc, seq in s.ops[e]:
                for k, v in waits:
                    if k[0] == 'e':
                        needed[k[1]].add(v)
        rank = {e: {v: i + 1 for i, v in enumerate(sorted(needed[e]))} for e in ENG}
        with nc.Block() as block:
            def mk(e):
                def body(eng):
                    esem = s.semh[('e', e)]
                    need = needed[e]
                    for fn, waits, inc, seq in s.ops[e]:
                        for k, v in waits:
                            eng.wait_ge(s.semh[k], rank[k[1]][v] if k[0] == 'e' else v)
                        if fn is None:
                            continue
                        ins = fn(eng)
                        if inc is None:
                            if seq in need:
                                ins.then_inc(esem, 1)
                        else:
                            ins.then_inc(s.semh[inc], 16)
                return body
            block.tensor(mk('pe'))
            block.scalar(mk('act'))
            block.vector(mk('dve'))
            block.gpsimd(mk('pool'))
            block.sync(mk('sp'))

    @staticmethod
    def _ts(*vs):
        return [v.t for v in vs if isinstance(v, V)]

    sep = None
    _last_pe = (0, 128)

    def matmul(s, out, lhsT, rhs, start=True, stop=True):
        o, l, r = out.ap, lhsT.ap, rhs.ap
        cur = (int(l.base_partition()), int(l.shape[0]))
        if cur[1] < 128 and s._last_pe[1] < 128 and s._last_pe[0] != cur[0] and s.sep is not None:
            so, sl, sr = s.sep
            a, b, c = so.ap, sl.ap, sr.ap
            s.op('pe', lambda e: e.matmul(a, lhsT=b, rhs=c, start=True, stop=True),
                 reads=[sl.t, sr.t], writes=[])
        s._last_pe = cur
        s.op('pe', lambda e: e.matmul(o, lhsT=l, rhs=r, start=start, stop=stop),
             reads=[lhsT.t, rhs.t], writes=[out.t])

    def transpose(s, out, in_, ident):
        o, i, d = out.ap, in_.ap, ident.ap
        s.op('pe', lambda e: e.transpose(o, i, d), reads=[in_.t, ident.t], writes=[out.t])

    def act(s, out, in_, func, bias=0.0, scale=1.0, accum=None, eng='act'):
        o, i, b, sc = out.ap, in_.ap, _ap(bias), _ap(scale)
        ac = _ap(accum)
        w = [out.t] + ([accum.t] if accum is not None else [])
        if ac is None:
            fn = lambda e: e.activation(out=o, in_=i, func=func, bias=b, scale=sc)
        else:
            fn = lambda e: e.activation(out=o, in_=i, func=func, bias=b, scale=sc, accum_out=ac)
        s.op(eng, fn, reads=s._ts(in_, bias, scale), writes=w)

    def tt(s, eng, out, in0, in1, op):
        o, a, b = out.ap, in0.ap, in1.ap
        s.op(eng, lambda e: e.tensor_tensor(out=o, in0=a, in1=b, op=op),
             reads=[in0.t, in1.t], writes=[out.t])

    def ts(s, eng, out, in0, s1, s2, op0, op1=None, accum=None):
        o, a, x1, x2 = out.ap, in0.ap, _ap(s1), _ap(s2)
        kw = {}
        if op1 is not None:
            kw['op1'] = op1
        if accum is not None:
            kw['accum_out'] = accum.ap
        w = [out.t] + ([accum.t] if accum is not None else [])
        s.op(eng, lambda e: e.tensor_scalar(out=o, in0=a, scalar1=x1, scalar2=x2, op0=op0, **kw),
             reads=s._ts(in0, s1, s2), writes=w)

    def stt(s, eng, out, in0, scalar, in1, op0, op1):
        o, a, sc, b = out.ap, in0.ap, _ap(scalar), in1.ap
        s.op(eng, lambda e: e.scalar_tensor_tensor(out=o, in0=a, scalar=sc, in1=b, op0=op0, op1=op1),
             reads=s._ts(in0, scalar, in1), writes=[out.t])

    def copy(s, eng, out, in_):
        o, i = out.ap, in_.ap
        if eng == 'act':
            s.op(eng, lambda e: e.copy(out=o, in_=i), reads=[in_.t], writes=[out.t])
        else:
            s.op(eng, lambda e: e.tensor_copy(out=o, in_=i), reads=[in_.t], writes=[out.t])

    def memset(s, eng, out, val):
        o = out.ap
        s.op(eng, lambda e: e.memset(o, val), reads=[], writes=[out.t])

    def recip(s, out, in_):
        o, i = out.ap, in_.ap
        s.op('dve', lambda e: e.reciprocal(out=o, in_=i), reads=[in_.t], writes=[out.t])

    def reduce(s, eng, out, in_, op, axis=None):
        o, i = out.ap, in_.ap
        ax = axis if axis is not None else AX.X
        s.op(eng, lambda e: e.tensor_reduce(out=o, in_=i, axis=ax, op=op), reads=[in_.t], writes=[out.t])


class Arena:
    def __init__(s, nc, es, nwords):
        s.h = es.enter_context(nc.sbuf_tensor('arena', [128, nwords], F32))
        s.n = nwords
        s.off = 0

    def mark(s):
        return s.off

    def reset(s, m=0):
        s.off = m

    def alloc(s, shape, dt=F32, name=''):
        n = int(np.prod(shape))
        words = n if dt == F32 else (n + 1) // 2
        words = (words + 7) // 8 * 8
        assert s.off + words <= s.n, 'SBUF arena overflow %s need %d have %d' % (name, words, s.n - s.off)
        ap = s.h[:, s.off:s.off + words]
        s.off += words
        if dt != F32:
            ap = ap.bitcast(dt)
        ap = ap[:, 0:n]
        if len(shape) > 1:
            names = ' '.join('a%d' % i for i in range(len(shape)))
            kw = {'a%d' % i: int(shape[i]) for i in range(1, len(shape))}
            ap = ap.rearrange('p (%s) -> p %s' % (names, names), **kw)
        return T(ap, name)


class Pool:
    def __init__(s, arena, n, shape, dt=F32, name=''):
        s.tiles = [arena.alloc(shape, dt, name + str(i)) for i in range(n)]
        s.i = 0

    def next(s):
        t = s.tiles[s.i % len(s.tiles)]
        s.i += 1
        return t


def run_threads(gens):
    gens = list(gens)
    while gens:
        for g in list(gens):
            try:
                next(g)
            except StopIteration:
                gens.remove(g)


def interleave(gens):
    gens = list(gens)
    while gens:
        for g in list(gens):
            try:
                next(g)
            except StopIteration:
                gens.remove(g)
        yield
import math
import ml_dtypes

DM = 1024
DFF = 4096
EVEN_IN = 3456
ODD_IN = 3600
EPS = 1e-6


class Cfg:
    def __init__(s, nseg=5, seglen=2048, ncores=8, debug=False, stop_after=None, skip=()):
        s.skip = skip
        s.nseg = nseg
        s.seglen = seglen
        s.ntok = nseg * seglen
        s.ncores = ncores
        s.debug = debug
        s.stop_after = stop_after
        s.ntile = s.ntok // 128
        s.tps = seglen // 128


def host_consts(cfg):
    c = {}
    c['ident'] = np.eye(128, dtype=np.float32)
    bo = np.zeros((128, 128), np.float32)
    bo[:64, :64] = 1.0
    bo[64:, 64:] = 1.0
    c['blockones'] = bo
    kp = np.arange(128)[:, None, None]
    j = np.arange(17)[None, :, None]
    qp = np.arange(128)[None, None, :]
    delta = 128 * (j - 8) + kp - qp
    ad = np.abs(delta).astype(np.float64)
    m = (ad <= 64).astype(np.float64) + ((delta % 4 == 0) & (ad <= 256)) + ((delta % 16 == 0) & (ad <= 1024))
    slopes = np.exp2(-8.0 * (np.arange(8) + 1.0) / 8)
    c['amask'] = np.stack([m * np.exp(-sl * ad) for sl in slopes], 0).astype(np.float32)
    i = np.arange(128)[:, None]
    jj = np.arange(128)[None, :]
    rm = np.zeros((2, 3, 128, 128), np.float32)
    rm[0, 0] = (i < jj); rm[0, 1] = (i <= jj); rm[0, 2] = (i > jj)
    rm[1, 0] = (i > jj); rm[1, 1] = (i >= jj); rm[1, 2] = (i < jj)
    c['rmask'] = rm
    tri = np.zeros((2, 128, 128), np.float32)
    tri[0] = (i <= jj); tri[1] = (i >= jj)
    c['tri'] = tri
    c['negmask'] = ((1.0 - tri) * -30000.0).astype(np.float32)
    return c


def rotary_tables(pos):
    half = 32
    inv = (np.float32(10000.0) ** (-np.arange(half, dtype=np.float32) / np.float32(half))).astype(np.float32)
    ang = pos.astype(np.float32)[None, :] * inv[:, None]
    cos = np.cos(ang).astype(np.float32)
    sin = np.sin(ang).astype(np.float32)
    p = np.arange(128)
    cos2 = cos[p % 32]
    sgn = np.where((p % 64) < 32, -1.0, 1.0).astype(np.float32)[:, None]
    sin2 = sin[p % 32] * sgn
    return np.ascontiguousarray(cos2), np.ascontiguousarray(sin2)


def swap_cols(w_in):
    j = np.arange(1024)
    src = (j // 64) * 64 + (j % 64 + 32) % 64
    return np.ascontiguousarray(w_in[:, src])


class K:
    def __init__(s, cfg):
        s.cfg = cfg
        s.es = ExitStack()
        s.nc = bass.Bass("TRN2", target_bir_lowering=False)
        s.P = Prog(s.nc, s.es)
        s.A = Arena(s.nc, s.es, 51 * 1024)
        s.ps = [T(s.es.enter_context(s.nc.psum_tensor('ps%d' % i, [128, 512], F32))[:, :], 'ps%d' % i)
                for i in range(8)]
        s.psi = 0
        s.din = {}
        s.dscr = {}
        s.outs = []

    def psum_g(s):
        while not s.ps_free:
            yield
        return s.ps_free.pop(0)

    def psrel(s, t):
        assert t not in s.ps_free
        s.ps_free.append(t)

    def psum(s):
        t = s.ps[s.psi % 5]
        s.psi += 1
        return t

    def inp(s, name, shape, dt=F32):
        h = s.nc.dram_tensor(name, list(shape), dt, kind="ExternalInput")
        d = D(h.ap(), name)
        s.din[name] = d
        return d

    def scratch(s, name, shape, dt):
        kind = "ExternalOutput" if s.cfg.debug else "Internal"
        h = s.nc.dram_tensor(name, list(shape), dt, kind=kind)
        d = D(h.ap(), name)
        s.dscr[name] = d
        return d

    def out(s, name, shape, dt=F32):
        h = s.nc.dram_tensor(name, list(shape), dt, kind="ExternalOutput")
        d = D(h.ap(), name)
        s.outs.append(d)
        return d

    def load_w_bf16(s, wd, kchunks, ncols, name):
        P = s.P
        wt = s.A.alloc([kchunks, ncols], BF16, name)
        src = wd.v.r('(k p) c -> p k c', p=128)
        for k in range(kchunks):
            P.dma('pool', wt[:, k, :], src[:, k, :])
        return wt

    def load_bcast(s, dv, n, name, eng='sp'):
        t = s.A.alloc([n], F32, name)
        s.P.dma(eng, t.v, V(dv.t, dv.ap.partition_broadcast(128)))
        return t

    def load_col(s, dv, name, reps=1):
        t = s.A.alloc([1], F32, name)
        n = dv.ap.shape[0]
        for r in range(reps):
            s.P.dma('sp', t[r * n:(r + 1) * n, :], V(dv.t, dv.ap.rearrange('(n o) -> n o', o=1)))
        return t

    def norm_transpose_block(s, src, tok0, gbc, uT, ident, xkeep=None):
        P = s.P
        for j in range(4):
            xt = s.xpool.next() if xkeep is None else xkeep[j]
            P.dma('sp', xt.v, src[tok0 + j * 128: tok0 + (j + 1) * 128, :])
            s.norm_transpose_tile(xt, j, gbc, uT, ident)

    def norm_transpose_tile(s, xt, j, gbc, uT, ident):
        P = s.P
        junk = s.junk.next()
        ss = s.small.next()
        P.act(junk.v, xt.v, AF.Square, accum=ss[:, 0:1])
        P.act(ss[:, 1:2], ss[:, 0:1], AF.Sqrt, scale=1.0 / DM, bias=s.epsc[:, 0:1])
        P.recip(ss[:, 2:3], ss[:, 1:2])
        u = s.upool.next()
        P.stt('dve', u.v, xt.v, ss[:, 2:3], gbc.v, ALU.mult, ALU.mult)
        pst = s.psum()
        pv = pst.v.bitcast(BF16).r('p (c t) -> p c t', c=8)
        for c in range(8):
            P.transpose(pv[:, c, :], u[:, c * 128:(c + 1) * 128], ident.v)
        P.copy('act', uT[:, :, j * 128:(j + 1) * 128], pv)

    def common_consts(s):
        P, A = s.P, s.A
        cd = s.inp('c_ident', [128, 128])
        s.ident = A.alloc([128], BF16, 'ident')
        P.dma('pool', s.ident.v, cd.v)
        cd = s.inp('c_blockones', [128, 128])
        s.blockones = A.alloc([128], BF16, 'blockones')
        P.dma('pool', s.blockones.v, cd.v)
        s.sepps = T(s.ps[5].ap, 'sepps')
        P.sep = (s.sepps[:, 511:512], s.ident[:, 0:128], s.ident[:, 0:1])
        s.epsc = A.alloc([4], F32, 'epsc')
        P.memset('dve', s.epsc[:, 0:1], EPS)
        P.memset('dve', s.epsc[:, 1:2], 1e-5)
        P.memset('dve', s.epsc[:, 2:3], 64e-5)
        P.memset('dve', s.epsc[:, 3:4], 1.0)

    def p1_even(s, src):
        cfg, P, A = s.cfg, s.P, s.A
        m0 = A.mark()
        W = s.load_w_bf16(s.W['even_in_w'], 8, EVEN_IN, 'w_in')
        gbc = s.load_bcast(s.W['norm_mix'][0], DM, 'gbc')
        qg = s.load_col(s.W['attn_q_gain'][0], 'qg', reps=2)
        kg = s.load_col(s.W['attn_k_gain'][0], 'kg', reps=2)
        P.ts('dve', qg.v, qg.v, 0.125, None, ALU.mult)
        s.xpool = Pool(A, 2, [DM], F32, 'x')
        s.junk = Pool(A, 1, [DM], BF16, 'junk')
        s.small = Pool(A, 4, [4], F32, 'small')
        s.upool = Pool(A, 2, [DM], BF16, 'u')
        uTp = Pool(A, 2, [8, 512], BF16, 'uT')
        sq = Pool(A, 2, [512], BF16, 'sq')
        rn = Pool(A, 2, [512], F32, 'rn')
        stg_b = Pool(A, 3, [512], BF16, 'stgb')
        stg_f = Pool(A, 3, [512], F32, 'stgf')
        QT, KT, VT, RW = s.dscr['QT'], s.dscr['KT'], s.dscr['Vtok'], s.dscr['RW']
        for blk in range(cfg.ntok // 512):
            tok0 = blk * 512
            uT = uTp.next()
            s.norm_transpose_block(src, tok0, gbc, uT, s.ident)
            for ct in range(27):
                if 8 <= ct < 12:
                    continue
                pt = s.psum()
                for k in range(8):
                    P.matmul(pt.v, W[:, k, ct * 128:(ct + 1) * 128], uT[:, k, :], start=(k == 0), stop=(k == 7))
                if ct < 8:
                    sqt = sq.next()
                    P.act(sqt.v, pt.v, AF.Square)
                    p2 = s.psum()
                    P.matmul(p2.v, s.blockones.v, sqt.v)
                    r = rn.next()
                    P.act(r.v, p2.v, AF.Sqrt, scale=1.0 / 64, bias=s.epsc[:, 0:1])
                    P.recip(r.v, r.v)
                    st = stg_b.next()
                    g = qg if ct < 4 else kg
                    P.stt('dve', st.v, pt.v, g[:, 0:1], r.v, ALU.mult, ALU.mult)
                    dst = QT if ct < 4 else KT
                    c0 = (ct % 4) * 128
                    P.dma('sp', dst[c0:c0 + 128, tok0:tok0 + 512], st.v)
                else:
                    st = stg_f.next()
                    P.copy('act' if ct % 2 else 'dve', st.v, pt.v)
                    c0 = (ct - 12) * 128
                    P.dma('sp', RW[c0:c0 + 128, tok0:tok0 + 512], st.v)
            for j in range(4):
                pt = s.psum()
                for k in range(8):
                    P.matmul(pt.v, uT[:, k, j * 128:(j + 1) * 128], W[:, k, 1024:1536], start=(k == 0), stop=(k == 7))
                st = stg_b.next()
                P.copy('act', st.v, pt.v)
                P.dma('sp', VT[tok0 + j * 128: tok0 + (j + 1) * 128, :], st.v)
        P.barrier()
        A.reset(m0)


    def attention(s):
        cfg, P, A = s.cfg, s.P, s.A
        m0 = A.mark()
        NT, ntile, tps = cfg.ntok, cfg.ntile, cfg.tps
        QT, KT, VT, YT = s.dscr['QT'], s.dscr['KT'], s.dscr['Vtok'], s.dscr['YT']
        cm = s.inp('c_amask', [8, 128, 17, 128])
        kt = A.alloc([NT], BF16, 'kt')
        qt = A.alloc([NT], BF16, 'qt')
        vt = A.alloc([ntile, 2, 65], BF16, 'vt')
        mk = A.alloc([2, 17, 128], BF16, 'mk')
        pex = Pool(A, 3, [4, 128], F32, 'pex')
        ptp = Pool(A, 3, [4, 128], BF16, 'ptp')
        osb = Pool(A, 2, [2, 64], BF16, 'osb')
        rin = Pool(A, 2, [2, 1], F32, 'rin')
        ost = Pool(A, 3, [128], BF16, 'ost')
        for hp in range(4):
            P.dma('sp', kt.v, KT[hp * 128:(hp + 1) * 128, :])
            P.dma('sp', qt.v, QT[hp * 128:(hp + 1) * 128, :])
            P.memset('pool', vt[:, :, :, 64:65], 1.0)
            for hh in range(2):
                P.dma('sp', vt[:, :, hh, 0:64],
                      VT.v.r('(n p) c -> p n c', p=128)[:, :, hp * 128 + hh * 64: hp * 128 + hh * 64 + 64])
                P.dma('pool', mk[:, hh, :, :], cm[2 * hp + hh])
            for qi in range(ntile):
                seg = qi // tps
                for hh in range(2):
                    ob = s.ps[6 + hh]
                    ov = ob[:, 0:65]
                    kjs = [kj for kj in range(qi - 8, qi + 9) if 0 <= kj < ntile and abs(kj // tps - seg) <= 1]
                    groups = []
                    for kj in kjs:
                        if groups and groups[-1][-1] // tps == kj // tps and len(groups[-1]) < 4:
                            groups[-1].append(kj)
                        else:
                            groups.append([kj])
                    nmm = 0
                    pend = None

                    def do_pv(pb, gkj, nmm):
                        for i, kj in enumerate(gkj):
                            P.matmul(ov, pb[:, i, :], vt[:, kj, hh, :], start=(nmm == 0), stop=(nmm == len(kjs) - 1))
                            nmm += 1
                        return nmm
                    for gkj in groups:
                        n = len(gkj)
                        pt = s.psum()
                        pv = pt.v.r('p (n q) -> p n q', n=4)
                        for i, kj in enumerate(gkj):
                            P.matmul(pv[:, i, :], kt[64 * hh:64 * hh + 64, kj * 128:(kj + 1) * 128],
                                     qt[64 * hh:64 * hh + 64, qi * 128:(qi + 1) * 128])
                        if pend is not None:
                            nmm = do_pv(pend[0], pend[1], nmm)
                        pe_ = pex.next()
                        P.act(pe_[:, 0:n, :], pv[:, 0:n, :], AF.Exp)
                        pb = ptp.next()
                        j0 = gkj[0] - qi + 8
                        kseg = gkj[0] // tps
                        if kseg == seg:
                            P.tt('dve', pb[:, 0:n, :], pe_[:, 0:n, :], mk[:, hh, j0:j0 + n, :], ALU.mult)
                        else:
                            b = seg if kseg < seg else seg + 1
                            P.stt('dve', pb[:, 0:n, :], pe_[:, 0:n, :], s.flags[:, b:b + 1], mk[:, hh, j0:j0 + n, :],
                                  ALU.mult, ALU.mult)
                        pend = (pb, gkj)
                    nmm = do_pv(pend[0], pend[1], nmm)
                r = rin.next()
                o = osb.next()
                for hh in range(2):
                    ob = s.ps[6 + hh]
                    P.recip(r[:, hh, :], ob[:, 64:65])
                    P.ts('dve', o[:, hh, :], ob[:, 0:64], r[:, hh, :], None, ALU.mult)
                pt = s.psum()
                pv = pt.v.bitcast(BF16)[:, 0:128]
                P.transpose(pv, o.v.r('p a b -> p (a b)'), s.ident.v)
                st = ost.next()
                P.copy('act', st.v, pv)
                P.dma('sp', YT[hp * 128:(hp + 1) * 128, qi * 128:(qi + 1) * 128], st.v)
        P.barrier()
        A.reset(m0)

    def rwkv(s):
        m0 = s.A.mark()
        try:
            s._rwkv()
        except StopIteration:
            s.P.barrier()
        s.A.reset(m0)

    def _rwkv(s):
        cfg, P, A = s.cfg, s.P, s.A
        m0 = A.mark()

        def chk(stage):
            if getattr(cfg, 'rw_stop', None) == stage:
                raise StopIteration
        NT, ntile, tps = cfg.ntok, cfg.ntile, cfg.tps
        RW, YT, YF = s.dscr['RW'], s.dscr['YT'], s.dscr['YF']
        W = s.W
        C0 = math.exp(-0.5)
        NB = 256
        NCH = NB // 128
        def cols(name, n, idx=None):
            t = A.alloc([n], F32, name)
            src = W[name].ap[0] if idx is None else W[name].ap[0][idx]
            P.dma('sp', t.v, V(W[name], src.rearrange('(c p) -> p c', p=128)), allow_slow_non_contiguous=True)
            return t
        mup = cols('rwkv_mu_prev', 15)
        mun = cols('rwkv_mu_next', 15)
        muc = A.alloc([15], F32, 'muc')
        P.tt('dve', muc.v, mup.v, mun.v, ALU.add)
        P.ts('dve', muc.v, muc.v, -1.0, 1.0, ALU.mult, ALU.add)
        kkc = cols('rwkv_k_k', 4)
        kac = cols('rwkv_k_a', 4)
        omka = A.alloc([4], F32, 'omka')
        P.ts('dve', omka.v, kac.v, -1.0, 1.0, ALU.mult, ALU.add)
        rkc = A.alloc([4], F32, 'rkc')
        P.dma('sp', rkc.v, V(W['rwkv_r_k'], W['rwkv_r_k'].ap[0].rearrange('(c h) e -> (h e) c', h=2)), allow_slow_non_contiguous=True)
        w0c = [cols('rwkv_w0', 4, d) for d in range(2)]
        a0c = [cols('rwkv_a0', 4, d) for d in range(2)]
        w2t = A.alloc([512], BF16, 'w2t')
        a2t = A.alloc([512], BF16, 'a2t')
        for d in range(2):
            P.dma('pool', w2t[64 * d:64 * d + 64, :], V(W['rwkv_w2'], W['rwkv_w2'].ap[0][d]))
            P.dma('pool', a2t[64 * d:64 * d + 64, :], V(W['rwkv_a2'], W['rwkv_a2'].ap[0][d]))
        g2t = A.alloc([512], BF16, 'g2t')
        P.dma('pool', g2t.v, V(W['rwkv_g2'], W['rwkv_g2'].ap[0]))
        lng = s.load_bcast(W['rwkv_ln_g'][0], 512, 'lng')
        lnb = s.load_bcast(W['rwkv_ln_b'][0], 512, 'lnb')
        cmk = s.inp('c_rmask', [2, 3, 128, 128])
        msk = A.alloc([2, 3, 128], BF16, 'rmask')
        for d in range(2):
            for i in range(3):
                P.dma('pool', msk[:, d, i, :], cmk[d, i])
        chk(1)
        rwb = A.alloc([15, NB + 2], F32, 'rwb')
        xs = A.alloc([15, NB], F32, 'xs')
        tmpA = Pool(A, 3, [4, NB], F32, 'tmpA')
        kk = A.alloc([4, NB], F32, 'kk')
        av = A.alloc([4, NB], F32, 'av')
        krep = A.alloc([4, NB], F32, 'krep')
        bv = A.alloc([4, NB], F32, 'bv')
        sg = A.alloc([4, NB], F32, 'sg')
        cumP = Pool(A, 2, [4, NB], F32, 'cum')
        Et = Pool(A, 1, [4, NB], F32, 'Et')
        osets = [dict(KR=A.alloc([4, NCH, 2, 128], BF16, 'KR%d' % i), BK=A.alloc([4, NCH, 2, 128], BF16, 'BK%d' % i),
                      tokm=A.alloc([NCH, 3, 512], BF16, 'tokm%d' % i), gT=A.alloc([4, NB], BF16, 'gT%d' % i),
                      bon=A.alloc([4, NB], BF16, 'bon%d' % i), Eplus=A.alloc([4, NCH], F32, 'gC%d' % i),
                      XA=[[A.alloc([2, 2, 128], BF16, 'XA') for c in range(4)] for n in range(NCH)],
                      XB=[[A.alloc([2, 2, 128], BF16, 'XB') for c in range(4)] for n in range(NCH)],
                      TT=[[A.alloc([2, 128], BF16, 'TT') for c in range(4)] for n in range(NCH)]) for i in range(2)]
        BKe = A.alloc([2, 4, NB], BF16, 'BKe')
        vbf = A.alloc([4, NB], BF16, 'vbf')
        bfA = Pool(A, 3, [NB], BF16, 'bfA')
        sgz_t = A.alloc([NB], BF16, 'sgz')
        Nn = [Pool(A, 2, [2, 128], BF16, 'Nn%d' % c) for c in range(4)]
        Nt = [Pool(A, 2, [2, 128], BF16, 'Nt%d' % c) for c in range(4)]
        Pp = [Pool(A, 2, [2, 128], BF16, 'Pp%d' % c) for c in range(4)]
        ST = A.alloc([4, 64], F32, 'ST')
        STb = A.alloc([4, 64], BF16, 'STb')
        Rn = Pool(A, 2, [8, 64], BF16, 'Rn')
        Ub = Pool(A, 2, [8, 64], BF16, 'Ub')
        yfp = Pool(A, 2, [8, 64], F32, 'yf')
        ysp = Pool(A, 2, [8, 64], F32, 'ys')
        st8 = Pool(A, 4, [8], F32, 'st8')
        ynb = Pool(A, 2, [512], BF16, 'ynb')
        fin = Pool(A, 2, [4, 128], F32, 'fin')
        finb = Pool(A, 2, [4, 128], BF16, 'finb')

        def hs_cumsum(src, d):
            cur = src
            sh = 1
            while sh < 128:
                nxt = cumP.next()
                cv = cur.v.r('p c (n t) -> p (c n) t', t=128)
                nv = nxt.v.r('p c (n t) -> p (c n) t', t=128)
                if d == 0:
                    P.tt('dve', nv[:, :, sh:], cv[:, :, sh:], cv[:, :, :128 - sh], ALU.add)
                    P.copy('pool', nv[:, :, :sh], cv[:, :, :sh])
                else:
                    P.tt('dve', nv[:, :, :128 - sh], cv[:, :, :128 - sh], cv[:, :, sh:], ALU.add)
                    P.copy('pool', nv[:, :, 128 - sh:], cv[:, :, 128 - sh:])
                cur = nxt
                sh *= 2
            return cur

        s.ps_free = [s.ps[i] for i in (0, 1, 2, 3, 4, 6, 7)]
        for d in range(2):
            P.memset('dve', ST.v, 0.0)
            P.memset('dve', STb.v, 0.0)
            blocks = list(range(NT // NB))
            if d == 1:
                blocks = blocks[::-1]

            def pre(blk, o, d=d):
                KR, BK, tokm, gT, bon = o['KR'], o['BK'], o['tokm'], o['gT'], o['bon']
                tok0 = blk * NB
                seg = tok0 // cfg.seglen
                lo, hi = tok0 - 1, tok0 + NB + 1
                if lo < 0:
                    P.memset('dve', rwb[:, :, 0:1], 0.0)
                    P.dma('sp', rwb[:, :, 1:], RW.v.r('(c p) t -> p c t', p=128)[:, :, 0:hi])
                elif hi > NT:
                    P.memset('dve', rwb[:, :, NB + 1:NB + 2], 0.0)
                    P.dma('sp', rwb[:, :, 0:NB + 1], RW.v.r('(c p) t -> p c t', p=128)[:, :, lo:NT])
                else:
                    P.dma('sp', rwb.v, RW.v.r('(c p) t -> p c t', p=128)[:, :, lo:hi])
                if tok0 % cfg.seglen == 0 and seg > 0:
                    P.ts('dve', rwb[:, :, 0:1], rwb[:, :, 0:1], s.flags[:, seg:seg + 1], None, ALU.mult)
                if (tok0 + NB) % cfg.seglen == 0 and seg + 1 < cfg.nseg:
                    P.ts('dve', rwb[:, :, NB + 1:NB + 2], rwb[:, :, NB + 1:NB + 2], s.flags[:, seg + 1:seg + 2], None, ALU.mult)
                for c in range(15):
                    t1 = tmpA.next()
                    X = rwb[:, c, 1:NB + 1]
                    P.act(t1[:, 0, :], X, AF.Identity, scale=muc[:, c:c + 1])
                    P.stt('dve', t1[:, 1, :], rwb[:, c, 0:NB], mup[:, c:c + 1], t1[:, 0, :], ALU.mult, ALU.add)
                    P.stt('dve', xs[:, c, :], rwb[:, c, 2:NB + 2], mun[:, c:c + 1], t1[:, 1, :], ALU.mult, ALU.add)
                    yield
                chk(2)
                r_, k_, v_ = xs[:, 0:4, :], xs[:, 4:8, :], xs[:, 8:12, :]
                for c in range(4):
                    t1 = tmpA.next()
                    P.act(t1[:, 0, :], xs[:, 4 + c, :], AF.Identity, scale=kkc[:, c:c + 1])
                    sq = bfA.next()
                    P.act(sq.v, t1[:, 0, :], AF.Square)
                    pt = yield from s.psum_g()
                    P.matmul(pt[:, 0:NB], s.blockones.v, sq.v)
                    P.act(t1[:, 1, :], pt[:, 0:NB], AF.Sqrt)
                    s.psrel(pt)
                    P.ts('dve', t1[:, 1, :], t1[:, 1, :], 1e-12, None, ALU.max)
                    P.recip(t1[:, 2, :], t1[:, 1, :])
                    P.tt('dve', kk[:, c, :], t1[:, 0, :], t1[:, 2, :], ALU.mult)
                    yield
                th = bfA.next()
                P.act(th[64 * d:64 * d + 64, :], xs[64 * d:64 * d + 64, 12, :], AF.Tanh)
                zab = bfA.next()
                P.copy('dve', zab[64 * d:64 * d + 64, :], xs[64 * d:64 * d + 64, 13, :])
                for c in range(4):
                    pt = yield from s.psum_g()
                    P.matmul(pt[:, 0:NB], w2t[64 * d:64 * d + 64, c * 128:(c + 1) * 128], th[64 * d:64 * d + 64, :])
                    P.act(sg[:, c, :], pt[:, 0:NB], AF.Sigmoid, bias=w0c[d][:, c:c + 1])
                    s.psrel(pt)
                    pt = yield from s.psum_g()
                    P.matmul(pt[:, 0:NB], a2t[64 * d:64 * d + 64, c * 128:(c + 1) * 128], zab[64 * d:64 * d + 64, :])
                    P.act(av[:, c, :], pt[:, 0:NB], AF.Sigmoid, bias=a0c[d][:, c:c + 1])
                    s.psrel(pt)
                    t1 = tmpA.next()
                    P.act(t1[:, 0, :], av[:, c, :], AF.Identity, scale=kac[:, c:c + 1], bias=omka[:, c:c + 1])
                    P.tt('dve', krep[:, c, :], xs[:, 4 + c, :], t1[:, 0, :], ALU.mult)
                    P.tt('pool', bv[:, c, :], av[:, c, :], kk[:, c, :], ALU.mult)
                    yield
                chk(3)
                cum = hs_cumsum(sg, d)
                yield
                endc = 127 if d == 0 else 0
                cumv = cum.v.r('p c (n t) -> p c n t', t=128)
                KRv, BKv = KR.v, BK.v
                t1 = tmpA.next()
                P.tt('dve', t1.v, cum.v, sg.v, ALU.subtract)
                E = Et.next()
                P.act(E.v, t1.v, AF.Exp, scale=-C0)
                P.tt('dve', KRv[:, :, :, 0, :], kk.v.r('p c (n t) -> p c n t', t=128), E.v.r('p c (n t) -> p c n t', t=128), ALU.mult)
                E = Et.next()
                P.act(E.v, cum.v, AF.Exp, scale=-C0)
                P.copy('pool', o['Eplus'].v, E.v.r('p c (n t) -> p c n t', t=128)[:, :, :, endc])
                yield
                P.tt('dve', KRv[:, :, :, 1, :], r_.r('p c (n t) -> p c n t', t=128), E.v.r('p c (n t) -> p c n t', t=128), ALU.mult)
                E2 = tmpA.next()
                P.act(E2.v, cum.v, AF.Exp, scale=C0)
                P.tt('dve', BKv[:, :, :, 0, :], bv.v.r('p c (n t) -> p c n t', t=128), E2.v.r('p c (n t) -> p c n t', t=128), ALU.mult)
                P.tt('pool', BKv[:, :, :, 1, :], krep.v.r('p c (n t) -> p c n t', t=128), E2.v.r('p c (n t) -> p c n t', t=128), ALU.mult)
                t1 = tmpA.next()
                P.tt('dve', t1.v.r('p c (n t) -> p c n t', t=128), cumv[:, :, :, endc:endc + 1].bc([128, 4, NCH, 128]), cumv, ALU.subtract)
                E3 = tmpA.next()
                P.act(E3.v, t1.v, AF.Exp, scale=-C0)
                P.tt('dve', BKe[:, 0, :, :], bv.v, E3.v, ALU.mult)
                P.tt('pool', BKe[:, 1, :, :], krep.v, E3.v, ALU.mult)
                P.copy('act', vbf.v, v_)
                yield
                chk(4)
                for n in range(NCH):
                    for qi, srcv in enumerate((vbf.v, BKe[:, 0, :, :], BKe[:, 1, :, :])):
                        pt = yield from s.psum_g()
                        pv = pt.v.bitcast(BF16)[:, 0:512].r('p (c x) -> p c x', c=4)
                        for c in range(4):
                            P.transpose(pv[:, c, :], srcv[:, c, n * 128:(n + 1) * 128], s.ident.v)
                        P.copy('act' if qi != 1 else 'dve', tokm[:, n, qi, :], pt.v.bitcast(BF16)[:, 0:512])
                        s.psrel(pt)
                        yield
                if d == 1:
                    sgz = sgz_t
                    P.act(sgz.v, xs[:, 14, :], AF.Sigmoid)
                    for c in range(4):
                        pt = yield from s.psum_g()
                        P.matmul(pt[:, 0:NB], g2t[:, c * 128:(c + 1) * 128], sgz.v)
                        P.copy('act', gT[:, c, :], pt[:, 0:NB])
                        s.psrel(pt)
                        rk = bfA.next()
                        P.stt('dve', rk.v, xs[:, c, :], rkc[:, c:c + 1], xs[:, 4 + c, :], ALU.mult, ALU.mult)
                        pt = yield from s.psum_g()
                        P.matmul(pt[:, 0:NB], s.blockones.v, rk.v)
                        P.tt('dve', bon[:, c, :], pt[:, 0:NB], xs[:, 8 + c, :], ALU.mult)
                        s.psrel(pt)
                        yield
                KRv, BKv = KR.v, BK.v
                for n in (range(NCH) if d == 0 else range(NCH - 1, -1, -1)):
                    def pair_chain(c, n=n, XA=o['XA'][n], XB=o['XB'][n], TT=o['TT'][n]):
                        px = yield from s.psum_g()
                        pxv = px.v.r('p (h a t) -> p h a t', h=2, a=2)
                        for hh in range(2):
                            pr = slice(64 * hh, 64 * hh + 64)
                            rhsKR = KRv[pr, c, n, :, :].r('p a t -> p (a t)')
                            P.matmul(pxv[:, hh, :, :].r('p a t -> p (a t)'), BKv[pr, c, n, 0, :], rhsKR)
                        yield
                        mS = msk[:, d, 0:2, :]
                        P.tt('dve', XA[c].v, pxv, V(mS.t, mS.ap.unsqueeze(1)).bc([128, 2, 2, 128]), ALU.mult)
                        s.psrel(px)
                        pz = yield from s.psum_g()
                        pzv = pz[:, 0:256].r('p (h t) -> p h t', h=2)
                        for hh in range(2):
                            pr = slice(64 * hh, 64 * hh + 64)
                            P.matmul(pzv[:, hh, :], KRv[pr, c, n, 0, :], BKv[pr, c, n, 0, :])
                        yield
                        mL = msk[:, d, 2, :]
                        nt_ = Nt[c].next()
                        P.tt('dve', nt_.v, pzv, V(mL.t, mL.ap.unsqueeze(1)).bc([128, 2, 128]), ALU.mult)
                        s.psrel(pz)
                        nn_ = XA[c][:, :, 0, :]
                        pp = Pp[c].next()
                        idv = s.ident.v
                        P.tt('pool', pp.v, V(idv.t, idv.ap.unsqueeze(1)).bc([128, 2, 128]), nn_, ALU.subtract)
                        py = yield from s.psum_g()
                        pyv = py.v.r('p (h a t) -> p h a t', h=2, a=2)
                        for hh in range(2):
                            pr = slice(64 * hh, 64 * hh + 64)
                            rhsKR = KRv[pr, c, n, :, :].r('p a t -> p (a t)')
                            P.matmul(pyv[:, hh, :, :].r('p a t -> p (a t)'), BKv[pr, c, n, 1, :], rhsKR)
                        yield
                        P.tt('dve', XB[c].v, pyv, V(mS.t, mS.ap.unsqueeze(1)).bc([128, 2, 2, 128]), ALU.mult)
                        s.psrel(py)
                        for lvl in range(1, 7):
                            last = (lvl == 6)
                            pa = yield from s.psum_g()
                            pav = pa.v.r('p (q h t) -> p q h t', q=2, h=2)
                            for hh in range(2):
                                P.matmul(pav[:, 0, hh, :], nn_[:, hh, :], nt_[:, hh, :])
                                if not last:
                                    P.matmul(pav[:, 1, hh, :], nt_[:, hh, :], nn_[:, hh, :])
                            yield
                            nt2 = Nt[c].next()
                            P.copy('act', nt2.v, pav[:, 0, :, :])
                            if not last:
                                nn2 = Nn[c].next()
                                P.copy('act', nn2.v, pav[:, 1, :, :])
                            s.psrel(pa)
                            yield
                            pb_ = yield from s.psum_g()
                            pbv = pb_[:, 0:256].r('p (h t) -> p h t', h=2)
                            for hh in range(2):
                                P.matmul(pbv[:, hh, :], nt2[:, hh, :], pp[:, hh, :])
                            yield
                            pp2 = TT[c] if last else Pp[c].next()
                            P.tt('dve', pp2.v, pp.v, pbv, ALU.add)
                            s.psrel(pb_)
                            pp = pp2
                            nt_ = nt2
                            if not last:
                                nn_ = nn2.v


                    yield from interleave([pair_chain(c) for c in range(4)])
                yield

            def chunks(blk, o, d=d):
                KR, BK, tokm, gT, bon, Eplus = o['KR'], o['BK'], o['tokm'], o['gT'], o['bon'], o['Eplus']
                KRv, BKv = KR.v, BK.v
                tok0 = blk * NB
                endc = 127 if d == 0 else 0
                chunks = list(range(NCH)) if d == 0 else list(range(NCH))[::-1]
                for n in chunks:
                    ctok = tok0 + n * 128
                    XA, XB, TT = o['XA'][n], o['XB'][n], o['TT'][n]
                    chk(6)
                    vt_ = tokm[:, n, 0, :].r('p (h e) -> p h e', h=8)
                    pr_ = yield from s.psum_g()
                    prv = pr_.v.r('p (h e) -> p h e', h=8)
                    for h in range(8):
                        c, hh = h // 2, h % 2
                        pr = slice(64 * hh, 64 * hh + 64)
                        P.matmul(prv[:, h, :], KRv[pr, c, n, 0, :], STb[pr, c, :], start=True, stop=False)
                        P.matmul(prv[:, h, :], XB[c][:, hh, 0, :], vt_[:, h, :], start=False, stop=True)
                    yield
                    rn = Rn.next()
                    P.act(rn.v, prv, AF.Identity, scale=-1.0)
                    s.psrel(pr_)
                    yield
                    pu = yield from s.psum_g()
                    puv = pu.v.r('p (h e) -> p h e', h=8)
                    for h in range(8):
                        c, hh = h // 2, h % 2
                        P.matmul(puv[:, h, :], TT[c][:, hh, :], rn[:, h, :])
                    yield
                    ub = Ub.next()
                    P.copy('act', ub.v, puv)
                    s.psrel(pu)
                    yield
                    py_ = yield from s.psum_g()
                    pyv_ = py_.v.r('p (h e) -> p h e', h=8)
                    for h in range(8):
                        c, hh = h // 2, h % 2
                        pr = slice(64 * hh, 64 * hh + 64)
                        P.matmul(pyv_[:, h, :], KRv[pr, c, n, 1, :], STb[pr, c, :], start=True, stop=False)
                        P.matmul(pyv_[:, h, :], XA[c][:, hh, 1, :], ub[:, h, :], start=False, stop=False)
                        P.matmul(pyv_[:, h, :], XB[c][:, hh, 1, :], vt_[:, h, :], start=False, stop=True)
                    yield
                    psn = yield from s.psum_g()
                    psv = psn.v.r('p (c h e) -> p c h e', c=4, h=2)
                    for h in range(8):
                        c, hh = h // 2, h % 2
                        P.matmul(psv[:, c, hh, :], tokm[:, n, 1, c * 128:(c + 1) * 128], ub[:, h, :], start=True, stop=False)
                        P.matmul(psv[:, c, hh, :], tokm[:, n, 2, c * 128:(c + 1) * 128], vt_[:, h, :], start=False, stop=True)
                    yield
                    gC = Eplus[:, :, n:n + 1]
                    P.tt('dve', ST.v, ST.v, gC.bc([128, 4, 64]), ALU.mult)
                    for hh in range(2):
                        pr = slice(64 * hh, 64 * hh + 64)
                        P.tt('dve', ST[pr, :, :], ST[pr, :, :], psv[pr, :, hh, :], ALU.add)
                    s.psrel(psn)
                    nxt_tok = ctok + 128 if d == 0 else ctok - 128
                    if 0 <= nxt_tok < NT and nxt_tok // cfg.seglen != ctok // cfg.seglen:
                        b = max(nxt_tok, ctok) // cfg.seglen
                        P.ts('dve', ST.v, ST.v, s.flags[:, b:b + 1], None, ALU.mult)
                    P.copy('pool', STb.v, ST.v)
                    yield
                    if d == 0:
                        yf = yfp.next()
                        P.copy('act', yf.v, pyv_)
                        s.psrel(py_)
                        P.dma('sp', YF[ctok:ctok + 128, :], yf.v.r('p h e -> p (h e)'))
                    else:
                        yf = yfp.next()
                        P.dma('sp', yf.v.r('p h e -> p (h e)'), YF[ctok:ctok + 128, :])
                        ys = ysp.next()
                        P.tt('dve', ys.v, yf.v, pyv_, ALU.add)
                        s.psrel(py_)
                        yield
                        if cfg.debug:
                            P.dma('sp', s.dscr['YS'][ctok:ctok + 128, :], ys.v.r('p h e -> p (h e)'))
                        s8 = st8.next()
                        P.reduce('dve', s8.v, ys.v, ALU.add)
                        P.ts('dve', s8.v, s8.v, 1.0 / 64, None, ALU.mult)
                        P.tt('dve', ys.v, ys.v, V(s8, s8.ap.unsqueeze(2)).bc([128, 8, 64]), ALU.subtract)
                        sq2 = yfp.next()
                        P.tt('pool', sq2.v, ys.v, ys.v, ALU.mult)
                        v8 = st8.next()
                        P.reduce('dve', v8.v, sq2.v, ALU.add)
                        P.act(v8.v, v8.v, AF.Sqrt, scale=1.0 / 64, bias=s.epsc[:, 2:3])
                        P.recip(v8.v, v8.v)
                        P.tt('dve', ys.v, ys.v, V(v8, v8.ap.unsqueeze(2)).bc([128, 8, 64]), ALU.mult)
                        ysf = ys.v.r('p h e -> p (h e)')
                        P.tt('pool', ysf, ysf, lng.v, ALU.mult)
                        yield
                        yb = ynb.next()
                        P.tt('pool', yb.v, ysf, lnb.v, ALU.add)
                        if cfg.debug:
                            P.dma('sp', s.dscr['YN'][ctok:ctok + 128, :], yb.v)
                        pt = yield from s.psum_g()
                        pv = pt.v.bitcast(BF16)[:, 0:512].r('p (c x) -> p c x', c=4)
                        for c in range(4):
                            P.transpose(pv[:, c, :], yb[:, c * 128:(c + 1) * 128], s.ident.v)
                        f1 = fin.next()
                        P.tt('dve', f1.v, pv, bon[:, :, n * 128:(n + 1) * 128], ALU.add)
                        s.psrel(pt)
                        f2 = finb.next()
                        P.tt('pool', f2.v, f1.v, gT[:, :, n * 128:(n + 1) * 128], ALU.mult)
                        P.dma('sp', YT.v.r('(c p) t -> p c t', p=128)[:, 4:8, ctok:ctok + 128], f2.v)

            def _drain(g):
                for _ in g:
                    pass
            _drain(pre(blocks[0], osets[0]))
            for bi, blk in enumerate(blocks):
                th = [chunks(blk, osets[bi % 2])]
                if bi + 1 < len(blocks):
                    th.append(pre(blocks[bi + 1], osets[(bi + 1) % 2]))
                if getattr(cfg, 'no_il', False):
                    for g_ in th:
                        _drain(g_)
                else:
                    run_threads(th)

            P.barrier()
        A.reset(m0)

    def p1_odd(s, src):
        cfg, P, A = s.cfg, s.P, s.A
        m0 = A.mark()
        W = s.load_w_bf16(s.W['odd_in_w'], 8, ODD_IN, 'w_in1')
        Wsw = s.load_w_bf16(s.W['odd_in_w_swap'], 8, 1024, 'w_sw')
        gbc = s.load_bcast(s.W['norm_mix'][1], DM, 'gbc')
        s.xpool = Pool(A, 2, [DM], F32, 'x')
        s.junk = Pool(A, 1, [DM], BF16, 'junk')
        s.small = Pool(A, 4, [4], F32, 'small')
        s.upool = Pool(A, 2, [DM], BF16, 'u')
        uTp = Pool(A, 2, [8, 512], BF16, 'uT')
        cosp = Pool(A, 2, [512], F32, 'cos')
        sinp = Pool(A, 2, [512], F32, 'sin')
        tA = Pool(A, 2, [512], F32, 'tA')
        tB = Pool(A, 2, [512], F32, 'tB')
        stg_b = Pool(A, 3, [512], BF16, 'stgb')
        stg_f = Pool(A, 3, [512], F32, 'stgf')
        stg_d = Pool(A, 2, [16], F32, 'stgd')
        ktk = Pool(A, 2, [4, 512], BF16, 'ktk')
        QT, KT, VT = s.dscr['QT'], s.dscr['KT'], s.dscr['Vtok']
        KTOK, GT, ZT, DT, XB = s.dscr['Ktok'], s.dscr['Gtok'], s.dscr['Ztok'], s.dscr['DTtok'], s.dscr['XBCT']
        ccos, csin = s.din['c_cos'], s.din['c_sin']
        for blk in range(cfg.ntok // 512):
            tok0 = blk * 512
            uT = uTp.next()
            s.norm_transpose_block(src, tok0, gbc, uT, s.ident)
            cs = cosp.next()
            sn = sinp.next()
            P.dma('sp', cs.v, ccos[:, tok0:tok0 + 512])
            P.dma('sp', sn.v, csin[:, tok0:tok0 + 512])
            kt_ = ktk.next()
            for ct in range(8):
                pt = s.psum()
                for k in range(8):
                    P.matmul(pt.v, W[:, k, ct * 128:(ct + 1) * 128], uT[:, k, :], start=(k == 0), stop=(k == 7))
                p2 = s.psum()
                for k in range(8):
                    P.matmul(p2.v, Wsw[:, k, ct * 128:(ct + 1) * 128], uT[:, k, :], start=(k == 0), stop=(k == 7))
                a = tA.next()
                b = tB.next()
                P.tt('dve', a.v, pt.v, cs.v, ALU.mult)
                P.tt('dve', b.v, p2.v, sn.v, ALU.mult)
                P.tt('pool', a.v, a.v, b.v, ALU.add)
                st = stg_b.next()
                P.act(st.v, a.v, AF.Identity, scale=(1.0 if ct < 4 else 0.125))
                dst = QT if ct < 4 else KT
                c0 = (ct % 4) * 128
                P.dma('sp', dst[c0:c0 + 128, tok0:tok0 + 512], st.v)
                if ct >= 4:
                    p3_ = s.psum()
                    pv = p3_.v.bitcast(BF16)[:, 0:512].r('p (j x) -> p j x', j=4)
                    for j in range(4):
                        P.transpose(pv[:, j, :], st[:, j * 128:(j + 1) * 128], s.ident.v)
                    P.copy('act', kt_[:, :, c0:c0 + 128], pv)
            P.dma('sp', KTOK.v.r('(n p) c -> p n c', p=128)[:, blk * 4:(blk + 1) * 4, :], kt_.v)
            for ct in range(8):
                pt = s.psum()
                for k in range(8):
                    P.matmul(pt.v, W[:, k, 2560 + ct * 128:2560 + (ct + 1) * 128], uT[:, k, :], start=(k == 0), stop=(k == 7))
                st = stg_f.next()
                P.copy('act' if ct % 2 else 'dve', st.v, pt.v)
                P.dma('sp', XB[ct * 128:(ct + 1) * 128, tok0:tok0 + 512], st.v)
            for j in range(4):
                lhs = lambda k: uT[:, k, j * 128:(j + 1) * 128]
                rows = slice(tok0 + j * 128, tok0 + (j + 1) * 128)
                for ci, (c0, dstd, isb) in enumerate(((1024, VT, True), (1536, GT, False), (2048, ZT, False))):
                    pt = s.psum()
                    for k in range(8):
                        P.matmul(pt.v, lhs(k), W[:, k, c0:c0 + 512], start=(k == 0), stop=(k == 7))
                    st = stg_b.next() if isb else stg_f.next()
                    P.copy('act' if ci % 2 else 'dve', st.v, pt.v)
                    P.dma('sp', dstd[rows, :], st.v)
                pt = s.psum()
                for k in range(8):
                    P.matmul(pt[:, 0:16], lhs(k), W[:, k, 3584:3600], start=(k == 0), stop=(k == 7))
                st = stg_d.next()
                P.copy('dve', st.v, pt[:, 0:16])
                P.dma('sp', DT[rows, :], st.v)
        P.barrier()
        A.reset(m0)

    def dattn(s):
        cfg, P, A = s.cfg, s.P, s.A
        m0 = A.mark()
        NT, ntile = cfg.ntok, cfg.ntile
        W = s.W
        QT, KT, VT = s.dscr['QT'], s.dscr['KT'], s.dscr['Vtok']
        KTOK, GT, ZT, DT, XB = s.dscr['Ktok'], s.dscr['Gtok'], s.dscr['Ztok'], s.dscr['DTtok'], s.dscr['XBCT']
        YT, YF2 = s.dscr['YT'], s.dscr['YF2']
        ctri = s.inp('c_tri', [2, 128, 128])
        cneg = s.inp('c_negmask', [2, 128, 128])
        tri = A.alloc([2, 128], F32, 'tri')
        neg = A.alloc([2, 128], F32, 'neg')
        for d in range(2):
            P.dma('sp', tri[:, d, :], ctri[d])
            P.dma('sp', neg[:, d, :], cneg[d])
        lgr = A.alloc([16], F32, 'lgr')
        P.dma('sp', lgr.v, V(W['ret_decay_exp'], W['ret_decay_exp'].ap[0].rearrange('a b -> (a b)').partition_broadcast(128)))
        P.act(lgr.v, lgr.v, AF.Exp, scale=-math.log(2.0))
        P.act(lgr.v, lgr.v, AF.Ln, scale=-1.0, bias=s.epsc[:, 3:4])
        gng = s.load_bcast(W['ret_gn_g'][0], 512, 'gng')
        gnb = s.load_bcast(W['ret_gn_b'][0], 512, 'gnb')
        dtb = A.alloc([16], F32, 'dtb')
        P.dma('sp', dtb.v, V(W['ssd_dt_bias'], W['ssd_dt_bias'].ap[0].rearrange('a b -> (a b)').partition_broadcast(128)))
        aneg = A.alloc([16], F32, 'aneg')
        P.dma('sp', aneg.v, V(W['ssd_a_log'], W['ssd_a_log'].ap[0].rearrange('a b -> (a b)').partition_broadcast(128)))
        P.act(aneg.v, aneg.v, AF.Exp)
        P.ts('dve', aneg.v, aneg.v, -1.0, None, ALU.mult)
        dsk = s.load_bcast(W['ssd_d'][0], 8, 'dsk')
        nrg = s.load_bcast(W['ssd_norm_g'][0], 512, 'nrg')
        cw = A.alloc([8, 5], F32, 'cw')
        for j in range(5):
            P.dma('sp', cw[:, :, j], V(W['ssd_conv_w'], W['ssd_conv_w'].ap[0][j].rearrange('(c p) -> p c', p=128)), allow_slow_non_contiguous=True)
        cb = A.alloc([8], F32, 'cb')
        P.dma('sp', cb.v, V(W['ssd_conv_b'], W['ssd_conv_b'].ap[0].rearrange('(c p) -> p c', p=128)), allow_slow_non_contiguous=True)
        def chk(st):
            if cfg.stop_after == st:
                raise StopIteration
        chk('d1')
        qTt = Pool(A, 2, [4, 128], BF16, 'qTt')
        kTt = Pool(A, 2, [4, 128], BF16, 'kTt')
        ktok = Pool(A, 2, [512], BF16, 'ktok')
        vtok = Pool(A, 2, [8, 64], BF16, 'vtok')
        xh = Pool(A, 2, [8, 132], F32, 'xh')
        cacc = Pool(A, 2, [8, 128], F32, 'cacc')
        ctmp = Pool(A, 2, [8, 128], F32, 'ctmp')
        xact = Pool(A, 2, [8, 128], BF16, 'xact')
        xstok = Pool(A, 3, [8, 64], BF16, 'xstok')
        bmtok = Pool(A, 2, [256], BF16, 'bmtok')
        dtt = Pool(A, 2, [16], F32, 'dtt')
        las = Pool(A, 2, [16], F32, 'las')
        vdir = Pool(A, 2, [8, 64], BF16, 'vdir')
        labc = Pool(A, 2, [8, 128], F32, 'labc')
        lapr = Pool(A, 2, [8, 64], F32, 'lapr')
        cumt = Pool(A, 2, [8], F32, 'cumt')
        dmt = Pool(A, 2, [8, 128], F32, 'dmt')
        crw = Pool(A, 2, [4, 128], F32, 'crw')
        PTp = Pool(A, 2, [8, 128], BF16, 'PTp')
        ecp = Pool(A, 2, [4, 128], F32, 'ecp')
        ecf = Pool(A, 2, [8, 128], F32, 'ecf')
        qtl = Pool(A, 2, [8, 128], BF16, 'qtl')
        wend = Pool(A, 2, [8], F32, 'wend')
        wv = Pool(A, 2, [8, 64], BF16, 'wv')
        Hr = A.alloc([4, 64], F32, 'Hr')
        Hrb = A.alloc([4, 64], BF16, 'Hrb')
        Hs = A.alloc([8, 64], F32, 'Hs')
        Hsb = A.alloc([8, 64], BF16, 'Hsb')
        yfp = Pool(A, 2, [1024], F32, 'yf2')
        ysp = Pool(A, 2, [1024], F32, 'ys2')
        sqp = Pool(A, 2, [512], F32, 'sq2')
        gz = Pool(A, 2, [1024], F32, 'gz')
        st8 = Pool(A, 4, [8], F32, 'st8b')
        ybf = Pool(A, 2, [1024], BF16, 'ybf')
        fout = Pool(A, 2, [8, 128], BF16, 'fout')

        def gen(m, d, la, qT_, kT_, ktok_, v_, H, Hb, R):
            endc = 127 if d == 0 else 0
            tri_d = tri[:, d, :]
            off = 0 if m == 'ret' else 512
            pc = yield from s.psum_g()
            P.matmul(pc[:, 0:8], tri_d, la)
            yield
            cum = cumt.next()
            P.copy('dve', cum.v, pc[:, 0:8])
            s.psrel(pc)
            lb = labc.next()
            P.copy('pool', lb.v, V(la.t, la.ap.unsqueeze(2)).bc([128, 8, 128]))
            dm = dmt.next()
            ef = ecf.next() if m == 'ssd' else None
            we = wend.next()
            yield
            for half in range(2):
                pr_ = yield from s.psum_g()
                prv = pr_.v.r('p (h l) -> p h l', h=4)
                for hq in range(4):
                    h = half * 4 + hq
                    P.matmul(prv[:, hq, :], lb[:, h, :], tri_d)
                yield
                hs_ = slice(half * 4, half * 4 + 4)
                if m == 'ssd':
                    cr = crw.next()
                    P.copy('dve', cr.v, prv)
                    s.psrel(pr_)
                    srcv = cr.v
                else:
                    srcv = prv
                P.tt('dve', dm[:, hs_, :], srcv, V(cum.t, cum.ap[:, hs_].unsqueeze(2)).bc([128, 4, 128]), ALU.subtract)
                if m == 'ssd':
                    P.act(ef[:, hs_, :], srcv, AF.Exp)
                P.tt('dve', we[:, hs_], srcv[:, :, endc], cum[:, hs_], ALU.subtract)
                if m != 'ssd':
                    s.psrel(pr_)
                yield
            ng = neg[:, d, :]
            P.tt('pool', dm.v, dm.v, V(ng.t, ng.ap.unsqueeze(1)).bc([128, 8, 128]), ALU.add)
            yield
            P.act(dm.v, dm.v, AF.Exp)
            P.act(we.v, we.v, AF.Exp)
            yield
            PT = PTp.next()
            ql = qtl.next()
            if m == 'ret':
                lp = lapr.next()
                P.copy('pool', lp.v, V(la.t, la.ap.unsqueeze(2)).bc([128, 8, 64]))
                yield
                pp_ = yield from s.psum_g()
                ppv = pp_.v.r('p (c l) -> p c l', c=4)
                lpv = lp.v.r('p (c hh) i -> p c (hh i)', hh=2)
                for c in range(4):
                    P.matmul(ppv[:, c, :], lpv[:, c, :], tri_d)
                yield
                ep = ecp.next()
                P.act(ep.v, ppv, AF.Exp)
                s.psrel(pp_)
                yield
                P.tt('dve', ql[:, 0:4, :], qT_.v, ep.v, ALU.mult)
                tot = ep[:, :, endc]
                for half in range(2):
                    ps_ = yield from s.psum_g()
                    psv = ps_.v.r('p (h l) -> p h l', h=4)
                    for hh in range(2):
                        for hq in range(4):
                            h = half * 4 + hq
                            if h % 2 != hh:
                                continue
                            c = h // 2
                            pr = slice(64 * hh, 64 * hh + 64)
                            P.matmul(psv[:, hq, :], kT_[pr, c, :], qT_[pr, c, :])
                    yield
                    hs_ = slice(half * 4, half * 4 + 4)
                    P.tt('dve', PT[:, hs_, :], psv, dm[:, hs_, :], ALU.mult)
                    s.psrel(ps_)
                    yield
                yb_ = yield from s.psum_g()
                ypsum = yb_.v.r('p (h e) -> p h e', h=8)
                for h in range(8):
                    c, hh = h // 2, h % 2
                    pr = slice(64 * hh, 64 * hh + 64)
                    P.matmul(ypsum[:, h, :], PT[:, h, :], v_[:, h, :], start=True, stop=False)
                    P.matmul(ypsum[:, h, :], ql[pr, c, :], Hb[pr, c, :], start=False, stop=True)
            else:
                P.tt('dve', ql.v.r('p (g q) l -> p g q l', g=2), V(qT_.t, qT_.ap.unsqueeze(2)).bc([128, 2, 4, 128]),
                     ef.v.r('p (g q) l -> p g q l', g=2), ALU.mult)
                tot = ef[:, :, endc]
                yield
                for g in range(2):
                    ps_ = yield from s.psum_g()
                    psv = ps_.v.r('p (h l) -> p h l', h=4)
                    for hq in range(4):
                        P.matmul(psv[:, hq, :], kT_[:, g, :], qT_[:, g, :])
                    yield
                    hs_ = slice(g * 4, g * 4 + 4)
                    P.tt('dve', PT[:, hs_, :], psv, dm[:, hs_, :], ALU.mult)
                    s.psrel(ps_)
                    yield
                yb_ = yield from s.psum_g()
                ypsum = yb_.v.r('p (h e) -> p h e', h=8)
                for h in range(8):
                    P.matmul(ypsum[:, h, :], PT[:, h, :], v_[:, h, :], start=True, stop=False)
                    P.matmul(ypsum[:, h, :], ql[:, h, :], Hb[:, h, :], start=False, stop=True)
            yield
            if d == 0:
                P.copy('act', R['yf'][:, off:off + 512], yb_.v)
            else:
                P.tt('dve', R['ys'][:, off:off + 512], R['yf'][:, off:off + 512], yb_.v, ALU.add)
            s.psrel(yb_)
            yield
            w_ = wv.next()
            P.tt('dve', w_.v, v_.v, V(we.t, we.ap.unsqueeze(2)).bc([128, 8, 64]), ALU.mult)
            yield
            ph = yield from s.psum_g()
            phv = ph.v.r('p (h e) -> p h e', h=8)
            for h in range(8):
                if m == 'ret':
                    c = h // 2
                    P.matmul(phv[:, h, :], ktok_[:, c * 128:(c + 1) * 128], w_[:, h, :])
                else:
                    g = h // 4
                    P.matmul(phv[:, h, :], ktok_[:, g * 128:(g + 1) * 128], w_[:, h, :])
            yield
            if m == 'ret':
                P.tt('dve', H.v, H.v, V(tot.t, tot.ap.unsqueeze(2)).bc([128, 4, 64]), ALU.mult)
                phv2 = ph.v.r('p (c hh e) -> p c hh e', c=4, hh=2)
                for hh in range(2):
                    pr = slice(64 * hh, 64 * hh + 64)
                    P.tt('dve', H[pr, :, :], H[pr, :, :], phv2[pr, :, hh, :], ALU.add)
            else:
                P.tt('dve', H.v, H.v, V(tot.t, tot.ap.unsqueeze(2)).bc([128, 8, 64]), ALU.mult)
                P.tt('dve', H.v, H.v, phv, ALU.add)
            s.psrel(ph)

        def prep(n, d, R):
            ctok = n * 128
            seg = ctok // cfg.seglen
            rows = slice(ctok, ctok + 128)
            R['rows'] = rows
            R['ctok'] = ctok
            qT_ = qTt.next()
            kT_ = kTt.next()
            kk_ = ktok.next()
            vv_ = vtok.next()
            P.dma('sp', qT_.v, QT.v.r('(c p) t -> p c t', p=128)[:, :, rows])
            P.dma('sp', kT_.v, KT.v.r('(c p) t -> p c t', p=128)[:, :, rows])
            P.dma('sp', kk_.v, KTOK[rows, :])
            P.dma('sp', vv_.v.r('p h e -> p (h e)'), VT[rows, :])
            R.update(qT=qT_, kT=kT_, kk=kk_, vv=vv_)
            yf = yfp.next()
            R['yf'] = yf
            if d == 1:
                P.dma('sp', yf.v, YF2[rows, :])
                R['ys'] = ysp.next()
            x_ = xh.next()
            lo, hi = ctok - 2, ctok + 130
            XBv = XB.v.r('(c p) t -> p c t', p=128)
            if lo < 0:
                P.memset('dve', x_[:, :, 0:2], 0.0)
                P.dma('sp', x_[:, :, 2:132], XBv[:, :, 0:hi])
            elif hi > NT:
                P.memset('dve', x_[:, :, 130:132], 0.0)
                P.dma('sp', x_[:, :, 0:130], XBv[:, :, lo:NT])
            else:
                P.dma('sp', x_.v, XBv[:, :, lo:hi])
            if ctok % cfg.seglen == 0 and seg > 0:
                P.ts('dve', x_[:, :, 0:2], x_[:, :, 0:2], s.flags[:, seg:seg + 1], None, ALU.mult)
            if (ctok + 128) % cfg.seglen == 0 and seg + 1 < cfg.nseg:
                P.ts('dve', x_[:, :, 130:132], x_[:, :, 130:132], s.flags[:, seg + 1:seg + 2], None, ALU.mult)
            yield
            acc = cacc.next()
            P.tt('dve', acc.v, x_[:, :, 0:128], cw[:, :, 0:1].bc([128, 8, 128]), ALU.mult)
            for j in range(1, 5):
                tm = ctmp.next()
                P.tt('pool', tm.v, x_[:, :, j:j + 128], cw[:, :, j:j + 1].bc([128, 8, 128]), ALU.mult)
                P.tt('dve', acc.v, acc.v, tm.v, ALU.add)
                yield
            P.tt('dve', acc.v, acc.v, V(cb, cb.ap.unsqueeze(2)).bc([128, 8, 128]), ALU.add)
            xa = xact.next()
            P.act(xa.v, acc.v, AF.Silu)
            yield
            xs_ = xstok.next()
            pt = yield from s.psum_g()
            pv = pt.v.bitcast(BF16)[:, 0:512].r('p (c x) -> p c x', c=4)
            for c in range(4):
                P.transpose(pv[:, c, :], xa[:, c, :], s.ident.v)
            yield
            P.copy('act', xs_.v.r('p h e -> p (h e)'), pt.v.bitcast(BF16)[:, 0:512])
            s.psrel(pt)
            bm_ = bmtok.next()
            pt = yield from s.psum_g()
            pv = pt.v.bitcast(BF16)[:, 0:256].r('p (c x) -> p c x', c=2)
            for c in range(2):
                P.transpose(pv[:, c, :], xa[:, 4 + c, :], s.ident.v)
            yield
            P.copy('act', bm_.v, pt.v.bitcast(BF16)[:, 0:256])
            s.psrel(pt)
            dt_ = dtt.next()
            P.dma('sp', dt_.v, DT[rows, :])
            P.tt('dve', dt_.v, dt_.v, dtb.v, ALU.add)
            yield
            P.act(dt_.v, dt_.v, AF.Exp)
            P.act(dt_.v, dt_.v, AF.Ln, bias=s.epsc[:, 3:4])
            yield
            la_ = las.next()
            P.tt('dve', la_.v, dt_.v, aneg.v, ALU.mult)
            vd = vdir.next()
            P.tt('dve', vd.v, xs_.v, V(dt_.t, dt_.ap[:, d * 8:(d + 1) * 8].unsqueeze(2)).bc([128, 8, 64]), ALU.mult)
            R.update(xa=xa, xs=xs_, bm=bm_, la=la_, vd=vd)

        def fin(R):
            rows = R['rows']
            ys, xs_ = R['ys'], R['xs']
            g_ = gz.next()
            P.dma('sp', g_[:, 0:512], GT[rows, :])
            P.dma('sp', g_[:, 512:1024], ZT[rows, :])
            P.act(g_.v, g_.v, AF.Silu)
            yield
            yr3 = ys[:, 0:512].r('p (h e) -> p h e', h=8)
            s8 = st8.next()
            P.reduce('dve', s8.v, yr3, ALU.add)
            P.ts('dve', s8.v, s8.v, 1.0 / 64, None, ALU.mult)
            P.tt('dve', yr3, yr3, V(s8, s8.ap.unsqueeze(2)).bc([128, 8, 64]), ALU.subtract)
            yield
            sq = sqp.next()
            P.tt('pool', sq.v, ys[:, 0:512], ys[:, 0:512], ALU.mult)
            yield
            v8 = st8.next()
            P.reduce('dve', v8.v, sq.v.r('p (h e) -> p h e', h=8), ALU.add)
            P.act(v8.v, v8.v, AF.Sqrt, scale=1.0 / 64, bias=s.epsc[:, 1:2])
            yield
            P.recip(v8.v, v8.v)
            P.tt('dve', yr3, yr3, V(v8, v8.ap.unsqueeze(2)).bc([128, 8, 64]), ALU.mult)
            yield
            P.tt('pool', ys[:, 0:512], ys[:, 0:512], gng.v, ALU.mult)
            P.tt('pool', ys[:, 0:512], ys[:, 0:512], gnb.v, ALU.add)
            yield
            yb = ybf.next()
            P.tt('dve', yb[:, 0:512], ys[:, 0:512], g_[:, 0:512], ALU.mult)
            sq = sqp.next()
            P.tt('dve', sq.v.r('p (h e) -> p h e', h=8), xs_.v, V(dsk, dsk.ap.unsqueeze(2)).bc([128, 8, 64]), ALU.mult)
            P.tt('dve', ys[:, 512:1024], ys[:, 512:1024], sq.v, ALU.add)
            yield
            P.tt('dve', ys[:, 512:1024], ys[:, 512:1024], g_[:, 512:1024], ALU.mult)
            sq = sqp.next()
            P.tt('pool', sq.v, ys[:, 512:1024], ys[:, 512:1024], ALU.mult)
            yield
            m8 = st8.next()
            P.reduce('dve', m8[:, 0:2], sq.v.r('p (g e) -> p g e', g=2), ALU.add)
            P.act(m8[:, 0:2], m8[:, 0:2], AF.Sqrt, scale=1.0 / 256, bias=s.epsc[:, 0:1])
            yield
            P.recip(m8[:, 0:2], m8[:, 0:2])
            yd2 = ys[:, 512:1024].r('p (g e) -> p g e', g=2)
            P.tt('dve', yd2, yd2, V(m8, m8.ap[:, 0:2].unsqueeze(2)).bc([128, 2, 256]), ALU.mult)
            P.tt('dve', yb[:, 512:1024], ys[:, 512:1024], nrg.v, ALU.mult)
            yield
            fo = fout.next()
            for half in range(2):
                pt = yield from s.psum_g()
                pv = pt.v.bitcast(BF16)[:, 0:512].r('p (c x) -> p c x', c=4)
                for c in range(4):
                    cc = half * 4 + c
                    P.transpose(pv[:, c, :], yb[:, cc * 128:(cc + 1) * 128], s.ident.v)
                yield
                P.copy('act', fo[:, half * 4:(half + 1) * 4, :], pv)
                s.psrel(pt)
            P.dma('sp', YT.v.r('(c p) t -> p c t', p=128)[:, :, rows], fo.v)

        def _drain(g):
            for _ in g:
                pass

        s.ps_free = [s.ps[i] for i in (0, 1, 2, 3, 4, 6, 7)]
        for d in range(2):
            for H_ in (Hr, Hrb, Hs, Hsb):
                P.memset('dve', H_.v, 0.0)
            order = list(range(ntile)) if d == 0 else list(range(ntile))[::-1]
            Rn_ = {}
            _drain(prep(order[0], d, Rn_))
            prevR = None
            for i, n in enumerate(order):
                R = Rn_
                ctok = R['ctok']
                cmT = V(R['xa'], R['xa'].ap[:, 6:8, :])
                bmT = V(R['xa'], R['xa'].ap[:, 4:6, :])
                th = [gen('ret', d, lgr[:, d * 8:(d + 1) * 8], R['qT'], R['kT'], R['kk'], R['vv'], Hr, Hrb, R),
                      gen('ssd', d, R['la'][:, d * 8:(d + 1) * 8], cmT, bmT, R['bm'], R['vd'], Hs, Hsb, R)]
                if i + 1 < len(order):
                    Rn_ = {}
                    th.append(prep(order[i + 1], d, Rn_))
                if d == 1 and prevR is not None:
                    th.append(fin(prevR))
                if getattr(cfg, 'no_il', False):
                    for g_ in th:
                        _drain(g_)
                else:
                    run_threads(th)
                nxt_tok = ctok + 128 if d == 0 else ctok - 128
                if 0 <= nxt_tok < NT and nxt_tok // cfg.seglen != ctok // cfg.seglen:
                    bnd = max(nxt_tok, ctok) // cfg.seglen
                    P.ts('dve', Hr.v, Hr.v, s.flags[:, bnd:bnd + 1], None, ALU.mult)
                    P.ts('dve', Hs.v, Hs.v, s.flags[:, bnd:bnd + 1], None, ALU.mult)
                P.copy('pool', Hrb.v, Hr.v)
                P.copy('pool', Hsb.v, Hs.v)
                if d == 0:
                    P.dma('sp', YF2[R['rows'], :], R['yf'].v)
                prevR = R
            if d == 1:
                _drain(fin(prevR))
            P.barrier()
        A.reset(m0)

    def p3(s, layer, src, dst, out_w):
        cfg, P, A = s.cfg, s.P, s.A
        m0 = A.mark()
        YT = s.dscr['YT']
        Wo = s.load_w_bf16(out_w, 8, DM, 'w_out')
        W1 = s.load_w_bf16(V(s.W['mlp_w1'], s.W['mlp_w1'].ap[layer]), 8, DFF, 'w1')
        W2 = s.load_w_bf16(V(s.W['mlp_w2'], s.W['mlp_w2'].ap[layer]), 32, DM, 'w2')
        gbc = s.load_bcast(s.W['norm_mlp'][layer], DM, 'gbc2')
        s.xpool = Pool(A, 2, [DM], F32, 'x')
        hpool = Pool(A, 2, [DM], F32, 'h')
        s.junk = Pool(A, 1, [DM], BF16, 'junk')
        s.small = Pool(A, 4, [4], F32, 'small')
        s.upool = Pool(A, 2, [DM], BF16, 'u')
        uTp = Pool(A, 1, [8, 256], BF16, 'uT')
        yTp = Pool(A, 2, [8, 256], BF16, 'yT')
        hid = A.alloc([32, 256], BF16, 'hid')
        rl = Pool(A, 2, [256], F32, 'rl')
        for blk in range(cfg.ntok // 256):
            tok0 = blk * 256
            yT = yTp.next()
            P.dma('sp', yT.v, YT.v.r('(k p) t -> p k t', p=128)[:, :, tok0:tok0 + 256])
            uT = uTp.next()
            hts = []
            for j in range(2):
                xt = s.xpool.next()
                P.dma('sp', xt.v, src[tok0 + j * 128: tok0 + (j + 1) * 128, :])
                ht = hpool.next()
                hts.append(ht)
                for half in range(2):
                    pt = s.psum()
                    for k in range(8):
                        P.matmul(pt.v, yT[:, k, j * 128:(j + 1) * 128], Wo[:, k, half * 512:(half + 1) * 512],
                                 start=(k == 0), stop=(k == 7))
                    P.tt('dve', ht[:, half * 512:(half + 1) * 512], xt[:, half * 512:(half + 1) * 512], pt.v, ALU.add)
                s.norm_transpose_tile(ht, j, gbc, uT, s.ident)
            for f in range(32):
                pt = s.psum()
                pv = pt[:, 0:256]
                for k in range(8):
                    P.matmul(pv, W1[:, k, f * 128:(f + 1) * 128], uT[:, k, :], start=(k == 0), stop=(k == 7))
                r = rl.next()
                P.act(r.v, pv, AF.Relu)
                P.tt('pool', hid[:, f, :], r.v, r.v, ALU.mult)
            for j in range(2):
                ht = hts[j]
                for half in range(2):
                    pt = s.psum()
                    for f in range(32):
                        P.matmul(pt.v, hid[:, f, j * 128:(j + 1) * 128], W2[:, f, half * 512:(half + 1) * 512],
                                 start=(f == 0), stop=(f == 31))
                    P.tt('dve', ht[:, half * 512:(half + 1) * 512], ht[:, half * 512:(half + 1) * 512], pt.v, ALU.add)
                P.dma('sp', dst[tok0 + j * 128: tok0 + (j + 1) * 128, :], ht.v)
        P.barrier()
        A.reset(m0)

    def build(s):
        cfg, P, A = s.cfg, s.P, s.A
        NT = cfg.ntok
        x = s.inp('x', [NT, DM])
        s.W = {}
        wshapes = dict(norm_mix=[2, DM], norm_mlp=[2, DM], mlp_w1=[2, DM, DFF], mlp_w2=[2, DFF, DM],
                       even_in_w=[DM, EVEN_IN], even_out_w=[DM, DM], attn_q_gain=[1, 64], attn_k_gain=[1, 64],
                       rwkv_mu_prev=[1, 1920], rwkv_mu_next=[1, 1920], rwkv_w0=[1, 2, 512], rwkv_w2=[1, 2, 64, 512],
                       rwkv_a0=[1, 2, 512], rwkv_a2=[1, 2, 64, 512], rwkv_g2=[1, 128, 512], rwkv_k_k=[1, 512],
                       rwkv_k_a=[1, 512], rwkv_r_k=[1, 8, 64], rwkv_ln_g=[1, 512], rwkv_ln_b=[1, 512],
                       odd_in_w=[DM, ODD_IN], odd_in_w_swap=[DM, 1024], odd_out_w=[DM, DM], ret_decay_exp=[1, 2, 8],
                       ret_gn_g=[1, 512], ret_gn_b=[1, 512], ssd_conv_w=[1, 5, 1024], ssd_conv_b=[1, 1024],
                       ssd_dt_bias=[1, 2, 8], ssd_a_log=[1, 2, 8], ssd_d=[1, 8], ssd_norm_g=[1, 512])
        for k, shp in wshapes.items():
            s.W[k] = s.inp(k, shp)
        s.scratch('QT', [512, NT], BF16)
        s.scratch('KT', [512, NT], BF16)
        s.scratch('Vtok', [NT, 512], BF16)
        s.scratch('RW', [1920, NT], F32)
        s.scratch('YT', [1024, NT], BF16)
        s.scratch('H1', [NT, DM], F32)
        s.scratch('YF', [NT, 512], F32)
        s.scratch('Ktok', [NT, 512], BF16)
        s.scratch('Gtok', [NT, 512], F32)
        s.scratch('Ztok', [NT, 512], F32)
        s.scratch('DTtok', [NT, 16], F32)
        s.scratch('XBCT', [1024, NT], F32)
        s.scratch('YF2', [NT, 1024], F32)
        s.inp('c_cos', [128, NT])
        s.inp('c_sin', [128, NT])
        if cfg.debug:
            s.scratch('YS', [NT, 512], F32)
            s.scratch('YN', [NT, 512], BF16)
        yout = s.out('y_out', [NT, DM])
        s.common_consts()
        fl = s.inp('flags', [128, 8])
        s.flags = A.alloc([8], F32, 'flags')
        P.dma('sp', s.flags.v, fl.v)
        s.p1_even(x.v)
        if cfg.stop_after == 'p1even':
            return s.finish()
        s.attention()
        if cfg.stop_after == 'attn':
            return s.finish()
        if 'rwkv' not in cfg.skip:
            s.rwkv()
        else:
            z = A.alloc([512], BF16, 'z')
            P.memset('dve', z.v, 0.0)
            for c in range(4, 8):
                for t0 in range(0, NT, 512):
                    P.dma('sp', s.dscr['YT'][c * 128:(c + 1) * 128, t0:t0 + 512], z.v)
        if cfg.stop_after == 'rwkv':
            return s.finish()
        s.p3(0, x.v, s.dscr['H1'].v, V(s.W['even_out_w'], s.W['even_out_w'].ap))
        if cfg.stop_after == 'l0':
            return s.finish()
        s.p1_odd(s.dscr['H1'].v)
        if cfg.stop_after == 'p1odd':
            return s.finish()
        try:
            s.dattn()
        except StopIteration:
            return s.finish()
        s.p3(1, s.dscr['H1'].v, yout.v, V(s.W['odd_out_w'], s.W['odd_out_w'].ap))
        return s.finish()

    def finish(s):
        s.P.barrier()
        s.P.emit()
        return s.nc
from concourse.bass_utils import run_bass_kernel_spmd


_CACHE = {}


def _assign():
    plan = []
    for c in range(8):
        if c < 2:
            segs = [('p', c, i) for i in range(4)] + [('s', c, 0)]
            fl = [0.0, 1.0, 1.0, 1.0, 0.0, 0.0, 0.0, 0.0]
        else:
            segs = [('s', 2 + 5 * (c - 2) + i, 0) for i in range(5)]
            fl = [0.0] * 8
        plan.append((segs, fl))
    return plan


def kernel(**inputs):
    cfg = Cfg(nseg=5, seglen=2048, ncores=8)
    if 'k' not in _CACHE:
        kb = K(cfg)
        kb.build()
        _CACHE['k'] = kb
    kb = _CACHE['k']
    nc = kb.nc
    xp = np.asarray(inputs['x_prompt'], dtype=np.float32)
    xs = np.asarray(inputs['x_sample'], dtype=np.float32)
    hc = host_consts(cfg)
    plan = _assign()
    w_sw = swap_cols(np.asarray(inputs['odd_in_w'], dtype=np.float32)[0])
    in_maps = []
    for c in range(8):
        segs, fl = plan[c]
        rows = []
        for kind, b, i in segs:
            rows.append(xp[b, i * 2048:(i + 1) * 2048] if kind == 'p' else xs[b])
        pos = np.concatenate([(i * 2048 + np.arange(2048)) if kind == 'p' else np.arange(2048)
                              for kind, b, i in segs])
        cos2, sin2 = rotary_tables(pos)
        m = {'x': np.ascontiguousarray(np.concatenate(rows, 0)), 'c_cos': cos2, 'c_sin': sin2,
             'flags': np.ascontiguousarray(np.broadcast_to(np.asarray(fl, np.float32)[None, :], (128, 8)))}
        for k in kb.W:
            if k == 'odd_in_w_swap':
                m[k] = w_sw
            else:
                m[k] = np.ascontiguousarray(np.asarray(inputs[k], dtype=np.float32)).reshape(kb.din[k].ap.shape)
        for k in kb.din:
            if k.startswith('c_') and k not in m:
                m[k] = hc[k[2:]]
        in_maps.append(m)
    res = run_bass_kernel_spmd(nc, in_maps, core_ids=list(range(8)))
    yp = np.zeros((2, 8192, 1024), np.float32)
    ys = np.zeros((32, 2048, 1024), np.float32)
    for c in range(8):
        o = np.asarray(res.results[c]['y_out'], dtype=np.float32)
        for j, (kind, b, i) in enumerate(plan[c][0]):
            blk = o[j * 2048:(j + 1) * 2048]
            if kind == 'p':
                yp[b, i * 2048:(i + 1) * 2048] = blk
            else:
                ys[b] = blk
    return (yp, ys)
```

```python
import numpy as np
from contextlib import ExitStack
import concourse.bass as bass
import concourse.mybir as mybir

F32 = mybir.dt.float32
BF16 = mybir.dt.bfloat16
AF = mybir.ActivationFunctionType
ALU = mybir.AluOpType
AX = mybir.AxisListType
ENG = ['pe', 'act', 'dve', 'pool', 'sp']


class T:
    _n = 0

    def __init__(s, ap, name=''):
        s.ap = ap
        s.lw = None
        s.rd = {}
        s.cnt = 0
        T._n += 1
        s.id = T._n
        s.name = name

    def __getitem__(s, k):
        return V(s, s.ap[k])

    @property
    def v(s):
        return V(s, s.ap)

    @property
    def t(s):
        return s


class D(T):
    def __init__(s, ap, name=''):
        T.__init__(s, ap, name)
        s.writes = {}
        s.reads = {}


class V:
    def __init__(s, t, ap):
        s.t = t
        s.ap = ap

    def __getitem__(s, k):
        return V(s.t, s.ap[k])

    def r(s, pat, **kw):
        return V(s.t, s.ap.rearrange(pat, **kw))

    def bc(s, shape):
        return V(s.t, s.ap.to_broadcast(list(shape)))

    def bitcast(s, dt):
        return V(s.t, s.ap.bitcast(dt))

    @property
    def shape(s):
        return s.ap.shape

    @property
    def v(s):
        return s


def _ap(x):
    return x.ap if isinstance(x, V) else x


class Prog:
    def __init__(s, nc, es):
        s.nc = nc
        s.es = es
        s.ops = {e: [] for e in ENG}
        s.seq = {e: 0 for e in ENG}
        s.known = {e: {} for e in ENG}
        s.semh = {}
        for e in ENG:
            s.semh[('e', e)] = es.enter_context(nc.semaphore('sem_' + e))
        s.dma_tiles = []
        s.slots = []
        s.free_slots = {'sw': [], 'hw': []}
        s.nops = 0

    def _emit(s, eng, fn, deps, inc, seq=None):
        if eng == 'pe':
            deps.pop(('e', 'pe'), None)
        kn = s.known[eng]
        waits = []
        for k, v in deps.items():
            if v <= kn.get(k, 0):
                continue
            kn[k] = v
            waits.append((k, v))
        s.ops[eng].append((fn, waits, inc, seq))
        s.nops += 1

    @staticmethod
    def _add(deps, d):
        if d is None:
            return
        k, v = d
        if deps.get(k, 0) < v:
            deps[k] = v

    NO_POOL = False

    def op(s, eng, fn, reads=(), writes=()):
        if eng == 'pool' and Prog.NO_POOL:
            eng = 'dve'
        deps = {}
        for t in reads:
            s._add(deps, t.lw)
        for t in writes:
            s._add(deps, t.lw)
            for d in t.rd.items():
                s._add(deps, d)
        s.seq[eng] += 1
        me = (('e', eng), s.seq[eng])
        s._emit(eng, fn, deps, None, me[1])
        for t in reads:
            if t.rd.get(me[0], 0) < me[1]:
                t.rd[me[0]] = me[1]
        for t in writes:
            t.lw = me
            t.rd = {}

    def _dsem(s, t, eng='sp'):
        cls = 'sw' if eng == 'pool' else 'hw'
        if not hasattr(t, 'slot') or t.slot is None:
            t.slot = {}
        if cls not in t.slot:
            fl = s.free_slots[cls]
            if fl:
                idx = fl.pop()
            else:
                idx = len(s.slots)
                h = s.es.enter_context(s.nc.semaphore('dsem%d' % idx))
                s.slots.append([h, 0, cls])
                s.semh[('d', idx)] = h
            t.slot[cls] = idx
            s.dma_tiles.append(t)
        return ('d', t.slot[cls])

    def dma(s, eng, out, in_, **kw):
        load = isinstance(in_.t, D)
        sb = out.t if load else in_.t
        dr = in_.t if load else out.t
        deps = {}
        if load:
            s._add(deps, sb.lw)
            for d in sb.rd.items():
                s._add(deps, d)
            for d in dr.writes.items():
                s._add(deps, d)
        else:
            s._add(deps, sb.lw)
            for d in dr.writes.items():
                s._add(deps, d)
            for d in dr.reads.items():
                s._add(deps, d)
        k = s._dsem(sb, eng)
        s.slots[k[1]][1] += 16
        cnt = s.slots[k[1]][1]
        me = (k, cnt)
        oa, ia = out.ap, in_.ap
        s._emit(eng, lambda e: e.dma_start(out=oa, in_=ia, **kw), deps, k)
        if load:
            sb.lw = me
            sb.rd = {}
            dr.reads[k] = cnt
        else:
            sb.rd[k] = cnt
            dr.writes[k] = cnt

    def barrier(s):
        deps = {}
        for e in ENG:
            if s.seq[e] > 0:
                deps[('e', e)] = s.seq[e]
        for idx, (h, c, _c) in enumerate(s.slots):
            if c > 0:
                deps[('d', idx)] = c
        for e in ENG:
            kn = s.known[e]
            waits = []
            for k, v in deps.items():
                if k == ('e', e) and e == 'pe':
                    continue
                if v <= kn.get(k, 0):
                    continue
                kn[k] = v
                waits.append((k, v))
            if waits:
                s.ops[e].append((None, waits, None, None))
        for t in s.dma_tiles:
            t.slot = None
        s.dma_tiles = []
        s.free_slots = {'sw': [i for i, x in enumerate(s.slots) if x[2] == 'sw'],
                        'hw': [i for i, x in enumerate(s.slots) if x[2] == 'hw']}

    def emit(s):
        nc = s.nc
        needed = {e: set() for e in ENG}
        for e in ENG:
            for fn, waits, inc, seq in s.ops[e]:
                for k, v in waits:
                    if k[0] == 'e':
                        needed[k[1]].add(v)
        rank = {e: {v: i + 1 for i, v in enumerate(sorted(needed[e]))} for e in ENG}
        with nc.Block() as block:
            def mk(e):
                def body(eng):
                    esem = s.semh[('e', e)]
                    need = needed[e]
                    for fn, waits, inc, seq in s.ops[e]:
                        for k, v in waits:
                            eng.wait_ge(s.semh[k], rank[k[1]][v] if k[0] == 'e' else v)
                        if fn is None:
                            continue
                        ins = fn(eng)
                        if inc is None:
                            if seq in need:
                                ins.then_inc(esem, 1)
                        else:
                            ins.then_inc(s.semh[inc], 16)
                return body
            block.tensor(mk('pe'))
            block.scalar(mk('act'))
            block.vector(mk('dve'))
            block.gpsimd(mk('pool'))
            block.sync(mk('sp'))

    @staticmethod
    def _ts(*vs):
        return [v.t for v in vs if isinstance(v, V)]

    sep = None
    _last_pe = (0, 128)

    def matmul(s, out, lhsT, rhs, start=True, stop=True):
        o, l, r = out.ap, lhsT.ap, rhs.ap
        cur = (int(l.base_partition()), int(l.shape[0]))
        if cur[1] < 128 and s._last_pe[1] < 128 and s._last_pe[0] != cur[0] and s.sep is not None:
            so, sl, sr = s.sep
            a, b, c = so.ap, sl.ap, sr.ap
            s.op('pe', lambda e: e.matmul(a, lhsT=b, rhs=c, start=True, stop=True),
                 reads=[sl.t, sr.t], writes=[])
        s._last_pe = cur
        s.op('pe', lambda e: e.matmul(o, lhsT=l, rhs=r, start=start, stop=stop),
             reads=[lhsT.t, rhs.t], writes=[out.t])

    def transpose(s, out, in_, ident):
        o, i, d = out.ap, in_.ap, ident.ap
        s.op('pe', lambda e: e.transpose(o, i, d), reads=[in_.t, ident.t], writes=[out.t])

    def act(s, out, in_, func, bias=0.0, scale=1.0, accum=None, eng='act'):
        o, i, b, sc = out.ap, in_.ap, _ap(bias), _ap(scale)
        ac = _ap(accum)
        w = [out.t] + ([accum.t] if accum is not None else [])
        if ac is None:
            fn = lambda e: e.activation(out=o, in_=i, func=func, bias=b, scale=sc)
        else:
            fn = lambda e: e.activation(out=o, in_=i, func=func, bias=b, scale=sc, accum_out=ac)
        s.op(eng, fn, reads=s._ts(in_, bias, scale), writes=w)

    def tt(s, eng, out, in0, in1, op):
        o, a, b = out.ap, in0.ap, in1.ap
        s.op(eng, lambda e: e.tensor_tensor(out=o, in0=a, in1=b, op=op),
             reads=[in0.t, in1.t], writes=[out.t])

    def ts(s, eng, out, in0, s1, s2, op0, op1=None, accum=None):
        o, a, x1, x2 = out.ap, in0.ap, _ap(s1), _ap(s2)
        kw = {}
        if op1 is not None:
            kw['op1'] = op1
        if accum is not None:
            kw['accum_out'] = accum.ap
        w = [out.t] + ([accum.t] if accum is not None else [])
        s.op(eng, lambda e: e.tensor_scalar(out=o, in0=a, scalar1=x1, scalar2=x2, op0=op0, **kw),
             reads=s._ts(in0, s1, s2), writes=w)

    def stt(s, eng, out, in0, scalar, in1, op0, op1):
        o, a, sc, b = out.ap, in0.ap, _ap(scalar), in1.ap
        s.op(eng, lambda e: e.scalar_tensor_tensor(out=o, in0=a, scalar=sc, in1=b, op0=op0, op1=op1),
             reads=s._ts(in0, scalar, in1), writes=[out.t])

    def copy(s, eng, out, in_):
        o, i = out.ap, in_.ap
        if eng == 'act':
            s.op(eng, lambda e: e.copy(out=o, in_=i), reads=[in_.t], writes=[out.t])
        else:
            s.op(eng, lambda e: e.tensor_copy(out=o, in_=i), reads=[in_.t], writes=[out.t])

    def memset(s, eng, out, val):
        o = out.ap
        s.op(eng, lambda e: e.memset(o, val), reads=[], writes=[out.t])

    def recip(s, out, in_):
        o, i = out.ap, in_.ap
        s.op('dve', lambda e: e.reciprocal(out=o, in_=i), reads=[in_.t], writes=[out.t])

    def reduce(s, eng, out, in_, op, axis=None):
        o, i = out.ap, in_.ap
        ax = axis if axis is not None else AX.X
        s.op(eng, lambda e: e.tensor_reduce(out=o, in_=i, axis=ax, op=op), reads=[in_.t], writes=[out.t])


class Arena:
    def __init__(s, nc, es, nwords):
        s.h = es.enter_context(nc.sbuf_tensor('arena', [128, nwords], F32))
        s.n = nwords
        s.off = 0

    def mark(s):
        return s.off

    def reset(s, m=0):
        s.off = m

    def alloc(s, shape, dt=F32, name=''):
        n = int(np.prod(shape))
        words = n if dt == F32 else (n + 1) // 2
        words = (words + 7) // 8 * 8
        assert s.off + words <= s.n, 'SBUF arena overflow %s need %d have %d' % (name, words, s.n - s.off)
        ap = s.h[:, s.off:s.off + words]
        s.off += words
        if dt != F32:
            ap = ap.bitcast(dt)
        ap = ap[:, 0:n]
        if len(shape) > 1:
            names = ' '.join('a%d' % i for i in range(len(shape)))
            kw = {'a%d' % i: int(shape[i]) for i in range(1, len(shape))}
            ap = ap.rearrange('p (%s) -> p %s' % (names, names), **kw)
        return T(ap, name)


class Pool:
    def __init__(s, arena, n, shape, dt=F32, name=''):
        s.tiles = [arena.alloc(shape, dt, name + str(i)) for i in range(n)]
        s.i = 0

    def next(s):
        t = s.tiles[s.i % len(s.tiles)]
        s.i += 1
        return t


def run_threads(gens):
    gens = list(gens)
    while gens:
        for g in list(gens):
            try:
                next(g)
            except StopIteration:
                gens.remove(g)


def interleave(gens):
    gens = list(gens)
    while gens:
        for g in list(gens):
            try:
                next(g)
            except StopIteration:
                gens.remove(g)
        yield
import math
import ml_dtypes

DM = 1024
DFF = 4096
EVEN_IN = 3456
ODD_IN = 3600
EPS = 1e-6


class Cfg:
    def __init__(s, nseg=5, seglen=2048, ncores=8, debug=False, stop_after=None, skip=()):
        s.skip = skip
        s.nseg = nseg
        s.seglen = seglen
        s.ntok = nseg * seglen
        s.ncores = ncores
        s.debug = debug
        s.stop_after = stop_after
        s.ntile = s.ntok // 128
        s.tps = seglen // 128


def host_consts(cfg):
    c = {}
    c['ident'] = np.eye(128, dtype=np.float32)
    bo = np.zeros((128, 128), np.float32)
    bo[:64, :64] = 1.0
    bo[64:, 64:] = 1.0
    c['blockones'] = bo
    kp = np.arange(128)[:, None, None]
    j = np.arange(17)[None, :, None]
    qp = np.arange(128)[None, None, :]
    delta = 128 * (j - 8) + kp - qp
    ad = np.abs(delta).astype(np.float64)
    m = (ad <= 64).astype(np.float64) + ((delta % 4 == 0) & (ad <= 256)) + ((delta % 16 == 0) & (ad <= 1024))
    slopes = np.exp2(-8.0 * (np.arange(8) + 1.0) / 8)
    c['amask'] = np.stack([m * np.exp(-sl * ad) for sl in slopes], 0).astype(np.float32)
    i = np.arange(128)[:, None]
    jj = np.arange(128)[None, :]
    rm = np.zeros((2, 3, 128, 128), np.float32)
    rm[0, 0] = (i < jj); rm[0, 1] = (i <= jj); rm[0, 2] = (i > jj)
    rm[1, 0] = (i > jj); rm[1, 1] = (i >= jj); rm[1, 2] = (i < jj)
    c['rmask'] = rm
    tri = np.zeros((2, 128, 128), np.float32)
    tri[0] = (i <= jj); tri[1] = (i >= jj)
    c['tri'] = tri
    c['negmask'] = ((1.0 - tri) * -30000.0).astype(np.float32)
    return c


def rotary_tables(pos):
    half = 32
    inv = (np.float32(10000.0) ** (-np.arange(half, dtype=np.float32) / np.float32(half))).astype(np.float32)
    ang = pos.astype(np.float32)[None, :] * inv[:, None]
    cos = np.cos(ang).astype(np.float32)
    sin = np.sin(ang).astype(np.float32)
    p = np.arange(128)
    cos2 = cos[p % 32]
    sgn = np.where((p % 64) < 32, -1.0, 1.0).astype(np.float32)[:, None]
    sin2 = sin[p % 32] * sgn
    return np.ascontiguousarray(cos2), np.ascontiguousarray(sin2)


def swap_cols(w_in):
    j = np.arange(1024)
    src = (j // 64) * 64 + (j % 64 + 32) % 64
    return np.ascontiguousarray(w_in[:, src])


class K:
    def __init__(s, cfg):
        s.cfg = cfg
        s.es = ExitStack()
        s.nc = bass.Bass("TRN2", target_bir_lowering=False)
        s.P = Prog(s.nc, s.es)
        s.A = Arena(s.nc, s.es, 51 * 1024)
        s.ps = [T(s.es.enter_context(s.nc.psum_tensor('ps%d' % i, [128, 512], F32))[:, :], 'ps%d' % i)
                for i in range(8)]
        s.psi = 0
        s.din = {}
        s.dscr = {}
        s.outs = []

    def psum_g(s):
        while not s.ps_free:
            yield
        return s.ps_free.pop(0)

    def psrel(s, t):
        assert t not in s.ps_free
        s.ps_free.append(t)

    def psum(s):
        t = s.ps[s.psi % 5]
        s.psi += 1
        return t

    def inp(s, name, shape, dt=F32):
        h = s.nc.dram_tensor(name, list(shape), dt, kind="ExternalInput")
        d = D(h.ap(), name)
        s.din[name] = d
        return d

    def scratch(s, name, shape, dt):
        kind = "ExternalOutput" if s.cfg.debug else "Internal"
        h = s.nc.dram_tensor(name, list(shape), dt, kind=kind)
        d = D(h.ap(), name)
        s.dscr[name] = d
        return d

    def out(s, name, shape, dt=F32):
        h = s.nc.dram_tensor(name, list(shape), dt, kind="ExternalOutput")
        d = D(h.ap(), name)
        s.outs.append(d)
        return d

    def load_w_bf16(s, wd, kchunks, ncols, name):
        P = s.P
        wt = s.A.alloc([kchunks, ncols], BF16, name)
        src = wd.v.r('(k p) c -> p k c', p=128)
        for k in range(kchunks):
            P.dma('pool', wt[:, k, :], src[:, k, :])
        return wt

    def load_bcast(s, dv, n, name, eng='sp'):
        t = s.A.alloc([n], F32, name)
        s.P.dma(eng, t.v, V(dv.t, dv.ap.partition_broadcast(128)))
        return t

    def load_col(s, dv, name, reps=1):
        t = s.A.alloc([1], F32, name)
        n = dv.ap.shape[0]
        for r in range(reps):
            s.P.dma('sp', t[r * n:(r + 1) * n, :], V(dv.t, dv.ap.rearrange('(n o) -> n o', o=1)))
        return t

    def norm_transpose_block(s, src, tok0, gbc, uT, ident, xkeep=None):
        P = s.P
        for j in range(4):
            xt = s.xpool.next() if xkeep is None else xkeep[j]
            P.dma('sp', xt.v, src[tok0 + j * 128: tok0 + (j + 1) * 128, :])
            s.norm_transpose_tile(xt, j, gbc, uT, ident)

    def norm_transpose_tile(s, xt, j, gbc, uT, ident):
        P = s.P
        junk = s.junk.next()
        ss = s.small.next()
        P.act(junk.v, xt.v, AF.Square, accum=ss[:, 0:1])
        P.act(ss[:, 1:2], ss[:, 0:1], AF.Sqrt, scale=1.0 / DM, bias=s.epsc[:, 0:1])
        P.recip(ss[:, 2:3], ss[:, 1:2])
        u = s.upool.next()
        P.stt('dve', u.v, xt.v, ss[:, 2:3], gbc.v, ALU.mult, ALU.mult)
        pst = s.psum()
        pv = pst.v.bitcast(BF16).r('p (c t) -> p c t', c=8)
        for c in range(8):
            P.transpose(pv[:, c, :], u[:, c * 128:(c + 1) * 128], ident.v)
        P.copy('act', uT[:, :, j * 128:(j + 1) * 128], pv)

    def common_consts(s):
        P, A = s.P, s.A
        cd = s.inp('c_ident', [128, 128])
        s.ident = A.alloc([128], BF16, 'ident')
        P.dma('pool', s.ident.v, cd.v)
        cd = s.inp('c_blockones', [128, 128])
        s.blockones = A.alloc([128], BF16, 'blockones')
        P.dma('pool', s.blockones.v, cd.v)
        s.sepps = T(s.ps[5].ap, 'sepps')
        P.sep = (s.sepps[:, 511:512], s.ident[:, 0:128], s.ident[:, 0:1])
        s.epsc = A.alloc([4], F32, 'epsc')
        P.memset('dve', s.epsc[:, 0:1], EPS)
        P.memset('dve', s.epsc[:, 1:2], 1e-5)
        P.memset('dve', s.epsc[:, 2:3], 64e-5)
        P.memset('dve', s.epsc[:, 3:4], 1.0)

    def p1_even(s, src):
        cfg, P, A = s.cfg, s.P, s.A
        m0 = A.mark()
        W = s.load_w_bf16(s.W['even_in_w'], 8, EVEN_IN, 'w_in')
        gbc = s.load_bcast(s.W['norm_mix'][0], DM, 'gbc')
        qg = s.load_col(s.W['attn_q_gain'][0], 'qg', reps=2)
        kg = s.load_col(s.W['attn_k_gain'][0], 'kg', reps=2)
        P.ts('dve', qg.v, qg.v, 0.125, None, ALU.mult)
        s.xpool = Pool(A, 2, [DM], F32, 'x')
        s.junk = Pool(A, 1, [DM], BF16, 'junk')
        s.small = Pool(A, 4, [4], F32, 'small')
        s.upool = Pool(A, 2, [DM], BF16, 'u')
        uTp = Pool(A, 2, [8, 512], BF16, 'uT')
        sq = Pool(A, 2, [512], BF16, 'sq')
        rn = Pool(A, 2, [512], F32, 'rn')
        stg_b = Pool(A, 3, [512], BF16, 'stgb')
        stg_f = Pool(A, 3, [512], F32, 'stgf')
        QT, KT, VT, RW = s.dscr['QT'], s.dscr['KT'], s.dscr['Vtok'], s.dscr['RW']
        for blk in range(cfg.ntok // 512):
            tok0 = blk * 512
            uT = uTp.next()
            s.norm_transpose_block(src, tok0, gbc, uT, s.ident)
            for ct in range(27):
                if 8 <= ct < 12:
                    continue
                pt = s.psum()
                for k in range(8):
                    P.matmul(pt.v, W[:, k, ct * 128:(ct + 1) * 128], uT[:, k, :], start=(k == 0), stop=(k == 7))
                if ct < 8:
                    sqt = sq.next()
                    P.act(sqt.v, pt.v, AF.Square)
                    p2 = s.psum()
                    P.matmul(p2.v, s.blockones.v, sqt.v)
                    r = rn.next()
                    P.act(r.v, p2.v, AF.Sqrt, scale=1.0 / 64, bias=s.epsc[:, 0:1])
                    P.recip(r.v, r.v)
                    st = stg_b.next()
                    g = qg if ct < 4 else kg
                    P.stt('dve', st.v, pt.v, g[:, 0:1], r.v, ALU.mult, ALU.mult)
                    dst = QT if ct < 4 else KT
                    c0 = (ct % 4) * 128
                    P.dma('sp', dst[c0:c0 + 128, tok0:tok0 + 512], st.v)
                else:
                    st = stg_f.next()
                    P.copy('act' if ct % 2 else 'dve', st.v, pt.v)
                    c0 = (ct - 12) * 128
                    P.dma('sp', RW[c0:c0 + 128, tok0:tok0 + 512], st.v)
            for j in range(4):
                pt = s.psum()
                for k in range(8):
                    P.matmul(pt.v, uT[:, k, j * 128:(j + 1) * 128], W[:, k, 1024:1536], start=(k == 0), stop=(k == 7))
                st = stg_b.next()
                P.copy('act', st.v, pt.v)
                P.dma('sp', VT[tok0 + j * 128: tok0 + (j + 1) * 128, :], st.v)
        P.barrier()
        A.reset(m0)


    def attention(s):
        cfg, P, A = s.cfg, s.P, s.A
        m0 = A.mark()
        NT, ntile, tps = cfg.ntok, cfg.ntile, cfg.tps
        QT, KT, VT, YT = s.dscr['QT'], s.dscr['KT'], s.dscr['Vtok'], s.dscr['YT']
        cm = s.inp('c_amask', [8, 128, 17, 128])
        kt = A.alloc([NT], BF16, 'kt')
        qt = A.alloc([NT], BF16, 'qt')
        vt = A.alloc([ntile, 2, 65], BF16, 'vt')
        mk = A.alloc([2, 17, 128], BF16, 'mk')
        pex = Pool(A, 3, [4, 128], BF16, 'pex')
        ptp = Pool(A, 3, [4, 128], BF16, 'ptp')
        osb = Pool(A, 2, [2, 64], BF16, 'osb')
        rin = Pool(A, 2, [2, 1], F32, 'rin')
        ost = Pool(A, 3, [128], BF16, 'ost')
        for hp in range(4):
            P.dma('sp', kt.v, KT[hp * 128:(hp + 1) * 128, :])
            P.dma('sp', qt.v, QT[hp * 128:(hp + 1) * 128, :])
            P.memset('pool', vt[:, :, :, 64:65], 1.0)
            for hh in range(2):
                P.dma('sp', vt[:, :, hh, 0:64],
                      VT.v.r('(n p) c -> p n c', p=128)[:, :, hp * 128 + hh * 64: hp * 128 + hh * 64 + 64])
                P.dma('pool', mk[:, hh, :, :], cm[2 * hp + hh])
            for qi in range(ntile):
                seg = qi // tps
                for hh in range(2):
                    ob = s.ps[6 + hh]
                    ov = ob[:, 0:65]
                    kjs = [kj for kj in range(qi - 8, qi + 9) if 0 <= kj < ntile and abs(kj // tps - seg) <= 1]
                    groups = []
                    for kj in kjs:
                        if groups and groups[-1][-1] // tps == kj // tps and len(groups[-1]) < 4:
                            groups[-1].append(kj)
                        else:
                            groups.append([kj])
                    nmm = 0
                    pend = None

                    def do_pv(pb, gkj, nmm):
                        for i, kj in enumerate(gkj):
                            P.matmul(ov, pb[:, i, :], vt[:, kj, hh, :], start=(nmm == 0), stop=(nmm == len(kjs) - 1))
                            nmm += 1
                        return nmm
                    for gkj in groups:
                        n = len(gkj)
                        pt = s.psum()
                        pv = pt.v.r('p (n q) -> p n q', n=4)
                        for i, kj in enumerate(gkj):
                            P.matmul(pv[:, i, :], kt[64 * hh:64 * hh + 64, kj * 128:(kj + 1) * 128],
                                     qt[64 * hh:64 * hh + 64, qi * 128:(qi + 1) * 128])
                        if pend is not None:
                            nmm = do_pv(pend[0], pend[1], nmm)
                        pe_ = pex.next()
                        P.act(pe_[:, 0:n, :], pv[:, 0:n, :], AF.Exp)
                        pb = ptp.next()
                        j0 = gkj[0] - qi + 8
                        kseg = gkj[0] // tps
                        if kseg == seg:
                            P.tt('dve', pb[:, 0:n, :], pe_[:, 0:n, :], mk[:, hh, j0:j0 + n, :], ALU.mult)
                        else:
                            b = seg if kseg < seg else seg + 1
                            P.stt('dve', pb[:, 0:n, :], pe_[:, 0:n, :], s.flags[:, b:b + 1], mk[:, hh, j0:j0 + n, :],
                                  ALU.mult, ALU.mult)
                        pend = (pb, gkj)
                    nmm = do_pv(pend[0], pend[1], nmm)
                r = rin.next()
                o = osb.next()
                for hh in range(2):
                    ob = s.ps[6 + hh]
                    P.recip(r[:, hh, :], ob[:, 64:65])
                    P.ts('dve', o[:, hh, :], ob[:, 0:64], r[:, hh, :], None, ALU.mult)
                pt = s.psum()
                pv = pt.v.bitcast(BF16)[:, 0:128]
                P.transpose(pv, o.v.r('p a b -> p (a b)'), s.ident.v)
                st = ost.next()
                P.copy('act', st.v, pv)
                P.dma('sp', YT[hp * 128:(hp + 1) * 128, qi * 128:(qi + 1) * 128], st.v)
        P.barrier()
        A.reset(m0)

    def rwkv(s):
        m0 = s.A.mark()
        try:
            s._rwkv()
        except StopIteration:
            s.P.barrier()
        s.A.reset(m0)

    def _rwkv(s):
        cfg, P, A = s.cfg, s.P, s.A
        m0 = A.mark()

        def chk(stage):
            if getattr(cfg, 'rw_stop', None) == stage:
                raise StopIteration
        NT, ntile, tps = cfg.ntok, cfg.ntile, cfg.tps
        RW, YT, YF = s.dscr['RW'], s.dscr['YT'], s.dscr['YF']
        W = s.W
        C0 = math.exp(-0.5)
        NB = 256
        NCH = NB // 128
        def cols(name, n, idx=None):
            t = A.alloc([n], F32, name)
            src = W[name].ap[0] if idx is None else W[name].ap[0][idx]
            P.dma('sp', t.v, V(W[name], src.rearrange('(c p) -> p c', p=128)), allow_slow_non_contiguous=True)
            return t
        mup = cols('rwkv_mu_prev', 15)
        mun = cols('rwkv_mu_next', 15)
        muc = A.alloc([15], F32, 'muc')
        P.tt('dve', muc.v, mup.v, mun.v, ALU.add)
        P.ts('dve', muc.v, muc.v, -1.0, 1.0, ALU.mult, ALU.add)
        kkc = cols('rwkv_k_k', 4)
        kac = cols('rwkv_k_a', 4)
        omka = A.alloc([4], F32, 'omka')
        P.ts('dve', omka.v, kac.v, -1.0, 1.0, ALU.mult, ALU.add)
        rkc = A.alloc([4], F32, 'rkc')
        P.dma('sp', rkc.v, V(W['rwkv_r_k'], W['rwkv_r_k'].ap[0].rearrange('(c h) e -> (h e) c', h=2)), allow_slow_non_contiguous=True)
        w0c = [cols('rwkv_w0', 4, d) for d in range(2)]
        a0c = [cols('rwkv_a0', 4, d) for d in range(2)]
        w2t = A.alloc([512], BF16, 'w2t')
        a2t = A.alloc([512], BF16, 'a2t')
        for d in range(2):
            P.dma('pool', w2t[64 * d:64 * d + 64, :], V(W['rwkv_w2'], W['rwkv_w2'].ap[0][d]))
            P.dma('pool', a2t[64 * d:64 * d + 64, :], V(W['rwkv_a2'], W['rwkv_a2'].ap[0][d]))
        g2t = A.alloc([512], BF16, 'g2t')
        P.dma('pool', g2t.v, V(W['rwkv_g2'], W['rwkv_g2'].ap[0]))
        lng = s.load_bcast(W['rwkv_ln_g'][0], 512, 'lng')
        lnb = s.load_bcast(W['rwkv_ln_b'][0], 512, 'lnb')
        cmk = s.inp('c_rmask', [2, 3, 128, 128])
        msk = A.alloc([2, 3, 128], BF16, 'rmask')
        for d in range(2):
            for i in range(3):
                P.dma('pool', msk[:, d, i, :], cmk[d, i])
        chk(1)
        rwb = A.alloc([15, NB + 2], F32, 'rwb')
        xs = A.alloc([15, NB], F32, 'xs')
        tmpA = Pool(A, 3, [4, NB], F32, 'tmpA')
        kk = A.alloc([4, NB], F32, 'kk')
        av = A.alloc([4, NB], F32, 'av')
        krep = A.alloc([4, NB], F32, 'krep')
        bv = A.alloc([4, NB], F32, 'bv')
        sg = A.alloc([4, NB], F32, 'sg')
        cumP = Pool(A, 2, [4, NB], F32, 'cum')
        Et = Pool(A, 1, [4, NB], F32, 'Et')
        osets = [dict(KR=A.alloc([4, NCH, 2, 128], BF16, 'KR%d' % i), BK=A.alloc([4, NCH, 2, 128], BF16, 'BK%d' % i),
                      tokm=A.alloc([NCH, 3, 512], BF16, 'tokm%d' % i), gT=A.alloc([4, NB], BF16, 'gT%d' % i),
                      bon=A.alloc([4, NB], BF16, 'bon%d' % i), Eplus=A.alloc([4, NCH], F32, 'gC%d' % i),
                      XA=[[A.alloc([2, 2, 128], BF16, 'XA') for c in range(4)] for n in range(NCH)],
                      XB=[[A.alloc([2, 2, 128], BF16, 'XB') for c in range(4)] for n in range(NCH)],
                      TT=[[A.alloc([2, 128], BF16, 'TT') for c in range(4)] for n in range(NCH)]) for i in range(2)]
        BKe = A.alloc([2, 4, NB], BF16, 'BKe')
        vbf = A.alloc([4, NB], BF16, 'vbf')
        bfA = Pool(A, 3, [NB], BF16, 'bfA')
        sgz_t = A.alloc([NB], BF16, 'sgz')
        Nn = [Pool(A, 2, [2, 128], BF16, 'Nn%d' % c) for c in range(4)]
        Nt = [Pool(A, 2, [2, 128], BF16, 'Nt%d' % c) for c in range(4)]
        Pp = [Pool(A, 2, [2, 128], BF16, 'Pp%d' % c) for c in range(4)]
        ST = A.alloc([4, 64], F32, 'ST')
        STb = A.alloc([4, 64], BF16, 'STb')
        Rn = Pool(A, 2, [8, 64], BF16, 'Rn')
        Ub = Pool(A, 2, [8, 64], BF16, 'Ub')
        yfp = Pool(A, 2, [8, 64], F32, 'yf')
        ysp = Pool(A, 2, [8, 64], F32, 'ys')
        st8 = Pool(A, 4, [8], F32, 'st8')
        ynb = Pool(A, 2, [512], BF16, 'ynb')
        fin = Pool(A, 2, [4, 128], F32, 'fin')
        finb = Pool(A, 2, [4, 128], BF16, 'finb')

        def hs_cumsum(src, d):
            cur = src
            sh = 1
            while sh < 128:
                nxt = cumP.next()
                cv = cur.v.r('p c (n t) -> p (c n) t', t=128)
                nv = nxt.v.r('p c (n t) -> p (c n) t', t=128)
                if d == 0:
                    P.tt('dve', nv[:, :, sh:], cv[:, :, sh:], cv[:, :, :128 - sh], ALU.add)
                    P.copy('pool', nv[:, :, :sh], cv[:, :, :sh])
                else:
                    P.tt('dve', nv[:, :, :128 - sh], cv[:, :, :128 - sh], cv[:, :, sh:], ALU.add)
                    P.copy('pool', nv[:, :, 128 - sh:], cv[:, :, 128 - sh:])
                cur = nxt
                sh *= 2
            return cur

        s.ps_free = [s.ps[i] for i in (0, 1, 2, 3, 4, 6, 7)]
        for d in range(2):
            P.memset('dve', ST.v, 0.0)
            P.memset('dve', STb.v, 0.0)
            blocks = list(range(NT // NB))
            if d == 1:
                blocks = blocks[::-1]

            def pre(blk, o, d=d):
                KR, BK, tokm, gT, bon = o['KR'], o['BK'], o['tokm'], o['gT'], o['bon']
                tok0 = blk * NB
                seg = tok0 // cfg.seglen
                lo, hi = tok0 - 1, tok0 + NB + 1
                if lo < 0:
                    P.memset('dve', rwb[:, :, 0:1], 0.0)
                    P.dma('sp', rwb[:, :, 1:], RW.v.r('(c p) t -> p c t', p=128)[:, :, 0:hi])
                elif hi > NT:
                    P.memset('dve', rwb[:, :, NB + 1:NB + 2], 0.0)
                    P.dma('sp', rwb[:, :, 0:NB + 1], RW.v.r('(c p) t -> p c t', p=128)[:, :, lo:NT])
                else:
                    P.dma('sp', rwb.v, RW.v.r('(c p) t -> p c t', p=128)[:, :, lo:hi])
                if tok0 % cfg.seglen == 0 and seg > 0:
                    P.ts('dve', rwb[:, :, 0:1], rwb[:, :, 0:1], s.flags[:, seg:seg + 1], None, ALU.mult)
                if (tok0 + NB) % cfg.seglen == 0 and seg + 1 < cfg.nseg:
                    P.ts('dve', rwb[:, :, NB + 1:NB + 2], rwb[:, :, NB + 1:NB + 2], s.flags[:, seg + 1:seg + 2], None, ALU.mult)
                for c in range(15):
                    t1 = tmpA.next()
                    X = rwb[:, c, 1:NB + 1]
                    P.act(t1[:, 0, :], X, AF.Identity, scale=muc[:, c:c + 1])
                    P.stt('dve', t1[:, 1, :], rwb[:, c, 0:NB], mup[:, c:c + 1], t1[:, 0, :], ALU.mult, ALU.add)
                    P.stt('dve', xs[:, c, :], rwb[:, c, 2:NB + 2], mun[:, c:c + 1], t1[:, 1, :], ALU.mult, ALU.add)
                    yield
                chk(2)
                r_, k_, v_ = xs[:, 0:4, :], xs[:, 4:8, :], xs[:, 8:12, :]
                for c in range(4):
                    t1 = tmpA.next()
                    P.act(t1[:, 0, :], xs[:, 4 + c, :], AF.Identity, scale=kkc[:, c:c + 1])
                    sq = bfA.next()
                    P.act(sq.v, t1[:, 0, :], AF.Square)
                    pt = yield from s.psum_g()
                    P.matmul(pt[:, 0:NB], s.blockones.v, sq.v)
                    P.act(t1[:, 1, :], pt[:, 0:NB], AF.Sqrt)
                    s.psrel(pt)
                    P.ts('dve', t1[:, 1, :], t1[:, 1, :], 1e-12, None, ALU.max)
                    P.recip(t1[:, 2, :], t1[:, 1, :])
                    P.tt('dve', kk[:, c, :], t1[:, 0, :], t1[:, 2, :], ALU.mult)
                    yield
                th = bfA.next()
                P.act(th[64 * d:64 * d + 64, :], xs[64 * d:64 * d + 64, 12, :], AF.Tanh)
                zab = bfA.next()
                P.copy('dve', zab[64 * d:64 * d + 64, :], xs[64 * d:64 * d + 64, 13, :])
                for c in range(4):
                    pt = yield from s.psum_g()
                    P.matmul(pt[:, 0:NB], w2t[64 * d:64 * d + 64, c * 128:(c + 1) * 128], th[64 * d:64 * d + 64, :])
                    P.act(sg[:, c, :], pt[:, 0:NB], AF.Sigmoid, bias=w0c[d][:, c:c + 1])
                    s.psrel(pt)
                    pt = yield from s.psum_g()
                    P.matmul(pt[:, 0:NB], a2t[64 * d:64 * d + 64, c * 128:(c + 1) * 128], zab[64 * d:64 * d + 64, :])
                    P.act(av[:, c, :], pt[:, 0:NB], AF.Sigmoid, bias=a0c[d][:, c:c + 1])
                    s.psrel(pt)
                    t1 = tmpA.next()
                    P.act(t1[:, 0, :], av[:, c, :], AF.Identity, scale=kac[:, c:c + 1], bias=omka[:, c:c + 1])
                    P.tt('dve', krep[:, c, :], xs[:, 4 + c, :], t1[:, 0, :], ALU.mult)
                    P.tt('pool', bv[:, c, :], av[:, c, :], kk[:, c, :], ALU.mult)
                    yield
                chk(3)
                cum = hs_cumsum(sg, d)
                yield
                endc = 127 if d == 0 else 0
                cumv = cum.v.r('p c (n t) -> p c n t', t=128)
                KRv, BKv = KR.v, BK.v
                t1 = tmpA.next()
                P.tt('dve', t1.v, cum.v, sg.v, ALU.subtract)
                E = Et.next()
                P.act(E.v, t1.v, AF.Exp, scale=-C0)
                P.tt('dve', KRv[:, :, :, 0, :], kk.v.r('p c (n t) -> p c n t', t=128), E.v.r('p c (n t) -> p c n t', t=128), ALU.mult)
                E = Et.next()
                P.act(E.v, cum.v, AF.Exp, scale=-C0)
                P.copy('pool', o['Eplus'].v, E.v.r('p c (n t) -> p c n t', t=128)[:, :, :, endc])
                yield
                P.tt('dve', KRv[:, :, :, 1, :], r_.r('p c (n t) -> p c n t', t=128), E.v.r('p c (n t) -> p c n t', t=128), ALU.mult)
                E2 = tmpA.next()
                P.act(E2.v, cum.v, AF.Exp, scale=C0)
                P.tt('dve', BKv[:, :, :, 0, :], bv.v.r('p c (n t) -> p c n t', t=128), E2.v.r('p c (n t) -> p c n t', t=128), ALU.mult)
                P.tt('pool', BKv[:, :, :, 1, :], krep.v.r('p c (n t) -> p c n t', t=128), E2.v.r('p c (n t) -> p c n t', t=128), ALU.mult)
                t1 = tmpA.next()
                P.tt('dve', t1.v.r('p c (n t) -> p c n t', t=128), cumv[:, :, :, endc:endc + 1].bc([128, 4, NCH, 128]), cumv, ALU.subtract)
                E3 = tmpA.next()
                P.act(E3.v, t1.v, AF.Exp, scale=-C0)
                P.tt('dve', BKe[:, 0, :, :], bv.v, E3.v, ALU.mult)
                P.tt('pool', BKe[:, 1, :, :], krep.v, E3.v, ALU.mult)
                P.copy('act', vbf.v, v_)
                yield
                chk(4)
                for n in range(NCH):
                    for qi, srcv in enumerate((vbf.v, BKe[:, 0, :, :], BKe[:, 1, :, :])):
                        pt = yield from s.psum_g()
                        pv = pt.v.bitcast(BF16)[:, 0:512].r('p (c x) -> p c x', c=4)
                        for c in range(4):
                            P.transpose(pv[:, c, :], srcv[:, c, n * 128:(n + 1) * 128], s.ident.v)
                        P.copy('act' if qi != 1 else 'dve', tokm[:, n, qi, :], pt.v.bitcast(BF16)[:, 0:512])
                        s.psrel(pt)
                        yield
                if d == 1:
                    sgz = sgz_t
                    P.act(sgz.v, xs[:, 14, :], AF.Sigmoid)
                    for c in range(4):
                        pt = yield from s.psum_g()
                        P.matmul(pt[:, 0:NB], g2t[:, c * 128:(c + 1) * 128], sgz.v)
                        P.copy('act', gT[:, c, :], pt[:, 0:NB])
                        s.psrel(pt)
                        rk = bfA.next()
                        P.stt('dve', rk.v, xs[:, c, :], rkc[:, c:c + 1], xs[:, 4 + c, :], ALU.mult, ALU.mult)
                        pt = yield from s.psum_g()
                        P.matmul(pt[:, 0:NB], s.blockones.v, rk.v)
                        P.tt('dve', bon[:, c, :], pt[:, 0:NB], xs[:, 8 + c, :], ALU.mult)
                        s.psrel(pt)
                        yield
                KRv, BKv = KR.v, BK.v
                for n in (range(NCH) if d == 0 else range(NCH - 1, -1, -1)):
                    def pair_chain(c, n=n, XA=o['XA'][n], XB=o['XB'][n], TT=o['TT'][n]):
                        px = yield from s.psum_g()
                        pxv = px.v.r('p (h a t) -> p h a t', h=2, a=2)
                        for hh in range(2):
                            pr = slice(64 * hh, 64 * hh + 64)
                            rhsKR = KRv[pr, c, n, :, :].r('p a t -> p (a t)')
                            P.matmul(pxv[:, hh, :, :].r('p a t -> p (a t)'), BKv[pr, c, n, 0, :], rhsKR)
                        yield
                        mS = msk[:, d, 0:2, :]
                        P.tt('dve', XA[c].v, pxv, V(mS.t, mS.ap.unsqueeze(1)).bc([128, 2, 2, 128]), ALU.mult)
                        s.psrel(px)
                        pz = yield from s.psum_g()
                        pzv = pz[:, 0:256].r('p (h t) -> p h t', h=2)
                        for hh in range(2):
                            pr = slice(64 * hh, 64 * hh + 64)
                            P.matmul(pzv[:, hh, :], KRv[pr, c, n, 0, :], BKv[pr, c, n, 0, :])
                        yield
                        mL = msk[:, d, 2, :]
                        nt_ = Nt[c].next()
                        P.tt('dve', nt_.v, pzv, V(mL.t, mL.ap.unsqueeze(1)).bc([128, 2, 128]), ALU.mult)
                        s.psrel(pz)
                        nn_ = XA[c][:, :, 0, :]
                        pp = Pp[c].next()
                        idv = s.ident.v
                        P.tt('pool', pp.v, V(idv.t, idv.ap.unsqueeze(1)).bc([128, 2, 128]), nn_, ALU.subtract)
                        py = yield from s.psum_g()
                        pyv = py.v.r('p (h a t) -> p h a t', h=2, a=2)
                        for hh in range(2):
                            pr = slice(64 * hh, 64 * hh + 64)
                            rhsKR = KRv[pr, c, n, :, :].r('p a t -> p (a t)')
                            P.matmul(pyv[:, hh, :, :].r('p a t -> p (a t)'), BKv[pr, c, n, 1, :], rhsKR)
                        yield
                        P.tt('dve', XB[c].v, pyv, V(mS.t, mS.ap.unsqueeze(1)).bc([128, 2, 2, 128]), ALU.mult)
                        s.psrel(py)
                        for lvl in range(1, 7):
                            last = (lvl == 6)
                            pa = yield from s.psum_g()
                            pav = pa.v.r('p (q h t) -> p q h t', q=2, h=2)
                            for hh in range(2):
                                P.matmul(pav[:, 0, hh, :], nn_[:, hh, :], nt_[:, hh, :])
                                if not last:
                                    P.matmul(pav[:, 1, hh, :], nt_[:, hh, :], nn_[:, hh, :])
                            yield
                            nt2 = Nt[c].next()
                            P.copy('act', nt2.v, pav[:, 0, :, :])
                            if not last:
                                nn2 = Nn[c].next()
                                P.copy('act', nn2.v, pav[:, 1, :, :])
                            s.psrel(pa)
                            yield
                            pb_ = yield from s.psum_g()
                            pbv = pb_[:, 0:256].r('p (h t) -> p h t', h=2)
                            for hh in range(2):
                                P.matmul(pbv[:, hh, :], nt2[:, hh, :], pp[:, hh, :])
                            yield
                            pp2 = TT[c] if last else Pp[c].next()
                            P.tt('dve', pp2.v, pp.v, pbv, ALU.add)
                            s.psrel(pb_)
                            pp = pp2
                            nt_ = nt2
                            if not last:
                                nn_ = nn2.v


                    yield from interleave([pair_chain(c) for c in range(4)])
                yield

            def chunks(blk, o, d=d):
                KR, BK, tokm, gT, bon, Eplus = o['KR'], o['BK'], o['tokm'], o['gT'], o['bon'], o['Eplus']
                KRv, BKv = KR.v, BK.v
                tok0 = blk * NB
                endc = 127 if d == 0 else 0
                chunks = list(range(NCH)) if d == 0 else list(range(NCH))[::-1]
                for n in chunks:
                    ctok = tok0 + n * 128
                    XA, XB, TT = o['XA'][n], o['XB'][n], o['TT'][n]
                    chk(6)
                    vt_ = tokm[:, n, 0, :].r('p (h e) -> p h e', h=8)
                    pr_ = yield from s.psum_g()
                    prv = pr_.v.r('p (h e) -> p h e', h=8)
                    for h in range(8):
                        c, hh = h // 2, h % 2
                        pr = slice(64 * hh, 64 * hh + 64)
                        P.matmul(prv[:, h, :], KRv[pr, c, n, 0, :], STb[pr, c, :], start=True, stop=False)
                        P.matmul(prv[:, h, :], XB[c][:, hh, 0, :], vt_[:, h, :], start=False, stop=True)
                    yield
                    rn = Rn.next()
                    P.act(rn.v, prv, AF.Identity, scale=-1.0)
                    s.psrel(pr_)
                    yield
                    pu = yield from s.psum_g()
                    puv = pu.v.r('p (h e) -> p h e', h=8)
                    for h in range(8):
                        c, hh = h // 2, h % 2
                        P.matmul(puv[:, h, :], TT[c][:, hh, :], rn[:, h, :])
                    yield
                    ub = Ub.next()
                    P.copy('act', ub.v, puv)
                    s.psrel(pu)
                    yield
                    py_ = yield from s.psum_g()
                    pyv_ = py_.v.r('p (h e) -> p h e', h=8)
                    for h in range(8):
                        c, hh = h // 2, h % 2
                        pr = slice(64 * hh, 64 * hh + 64)
                        P.matmul(pyv_[:, h, :], KRv[pr, c, n, 1, :], STb[pr, c, :], start=True, stop=False)
                        P.matmul(pyv_[:, h, :], XA[c][:, hh, 1, :], ub[:, h, :], start=False, stop=False)
                        P.matmul(pyv_[:, h, :], XB[c][:, hh, 1, :], vt_[:, h, :], start=False, stop=True)
                    yield
                    psn = yield from s.psum_g()
                    psv = psn.v.r('p (c h e) -> p c h e', c=4, h=2)
                    for h in range(8):
                        c, hh = h // 2, h % 2
                        P.matmul(psv[:, c, hh, :], tokm[:, n, 1, c * 128:(c + 1) * 128], ub[:, h, :], start=True, stop=False)
                        P.matmul(psv[:, c, hh, :], tokm[:, n, 2, c * 128:(c + 1) * 128], vt_[:, h, :], start=False, stop=True)
                    yield
                    gC = Eplus[:, :, n:n + 1]
                    P.tt('dve', ST.v, ST.v, gC.bc([128, 4, 64]), ALU.mult)
                    for hh in range(2):
                        pr = slice(64 * hh, 64 * hh + 64)
                        P.tt('dve', ST[pr, :, :], ST[pr, :, :], psv[pr, :, hh, :], ALU.add)
                    s.psrel(psn)
                    nxt_tok = ctok + 128 if d == 0 else ctok - 128
                    if 0 <= nxt_tok < NT and nxt_tok // cfg.seglen != ctok // cfg.seglen:
                        b = max(nxt_tok, ctok) // cfg.seglen
                        P.ts('dve', ST.v, ST.v, s.flags[:, b:b + 1], None, ALU.mult)
                    P.copy('pool', STb.v, ST.v)
                    yield
                    if d == 0:
                        yf = yfp.next()
                        P.copy('act', yf.v, pyv_)
                        s.psrel(py_)
                        P.dma('sp', YF[ctok:ctok + 128, :], yf.v.r('p h e -> p (h e)'))
                    else:
                        yf = yfp.next()
                        P.dma('sp', yf.v.r('p h e -> p (h e)'), YF[ctok:ctok + 128, :])
                        ys = ysp.next()
                        P.tt('dve', ys.v, yf.v, pyv_, ALU.add)
                        s.psrel(py_)
                        yield
                        if cfg.debug:
                            P.dma('sp', s.dscr['YS'][ctok:ctok + 128, :], ys.v.r('p h e -> p (h e)'))
                        s8 = st8.next()
                        P.reduce('dve', s8.v, ys.v, ALU.add)
                        P.ts('dve', s8.v, s8.v, 1.0 / 64, None, ALU.mult)
                        P.tt('dve', ys.v, ys.v, V(s8, s8.ap.unsqueeze(2)).bc([128, 8, 64]), ALU.subtract)
                        sq2 = yfp.next()
                        P.tt('pool', sq2.v, ys.v, ys.v, ALU.mult)
                        v8 = st8.next()
                        P.reduce('dve', v8.v, sq2.v, ALU.add)
                        P.act(v8.v, v8.v, AF.Sqrt, scale=1.0 / 64, bias=s.epsc[:, 2:3])
                        P.recip(v8.v, v8.v)
                        P.tt('dve', ys.v, ys.v, V(v8, v8.ap.unsqueeze(2)).bc([128, 8, 64]), ALU.mult)
                        ysf = ys.v.r('p h e -> p (h e)')
                        P.tt('pool', ysf, ysf, lng.v, ALU.mult)
                        yield
                        yb = ynb.next()
                        P.tt('pool', yb.v, ysf, lnb.v, ALU.add)
                        if cfg.debug:
                            P.dma('sp', s.dscr['YN'][ctok:ctok + 128, :], yb.v)
                        pt = yield from s.psum_g()
                        pv = pt.v.bitcast(BF16)[:, 0:512].r('p (c x) -> p c x', c=4)
                        for c in range(4):
                            P.transpose(pv[:, c, :], yb[:, c * 128:(c + 1) * 128], s.ident.v)
                        f1 = fin.next()
                        P.tt('dve', f1.v, pv, bon[:, :, n * 128:(n + 1) * 128], ALU.add)
                        s.psrel(pt)
                        f2 = finb.next()
                        P.tt('pool', f2.v, f1.v, gT[:, :, n * 128:(n + 1) * 128], ALU.mult)
                        P.dma('sp', YT.v.r('(c p) t -> p c t', p=128)[:, 4:8, ctok:ctok + 128], f2.v)

            def _drain(g):
                for _ in g:
                    pass
            _drain(pre(blocks[0], osets[0]))
            for bi, blk in enumerate(blocks):
                th = [chunks(blk, osets[bi % 2])]
                if bi + 1 < len(blocks):
                    th.append(pre(blocks[bi + 1], osets[(bi + 1) % 2]))
                if getattr(cfg, 'no_il', False):
                    for g_ in th:
                        _drain(g_)
                else:
                    run_threads(th)

            P.barrier()
        A.reset(m0)

    def p1_odd(s, src):
        cfg, P, A = s.cfg, s.P, s.A
        m0 = A.mark()
        W = s.load_w_bf16(s.W['odd_in_w'], 8, ODD_IN, 'w_in1')
        Wsw = s.load_w_bf16(s.W['odd_in_w_swap'], 8, 1024, 'w_sw')
        gbc = s.load_bcast(s.W['norm_mix'][1], DM, 'gbc')
        s.xpool = Pool(A, 2, [DM], F32, 'x')
        s.junk = Pool(A, 1, [DM], BF16, 'junk')
        s.small = Pool(A, 4, [4], F32, 'small')
        s.upool = Pool(A, 2, [DM], BF16, 'u')
        uTp = Pool(A, 2, [8, 512], BF16, 'uT')
        cosp = Pool(A, 2, [512], F32, 'cos')
        sinp = Pool(A, 2, [512], F32, 'sin')
        tA = Pool(A, 2, [512], F32, 'tA')
        tB = Pool(A, 2, [512], F32, 'tB')
        stg_b = Pool(A, 3, [512], BF16, 'stgb')
        stg_f = Pool(A, 3, [512], F32, 'stgf')
        stg_d = Pool(A, 2, [16], F32, 'stgd')
        ktk = Pool(A, 2, [4, 512], BF16, 'ktk')
        QT, KT, VT = s.dscr['QT'], s.dscr['KT'], s.dscr['Vtok']
        KTOK, GT, ZT, DT, XB = s.dscr['Ktok'], s.dscr['Gtok'], s.dscr['Ztok'], s.dscr['DTtok'], s.dscr['XBCT']
        ccos, csin = s.din['c_cos'], s.din['c_sin']
        for blk in range(cfg.ntok // 512):
            tok0 = blk * 512
            uT = uTp.next()
            s.norm_transpose_block(src, tok0, gbc, uT, s.ident)
            cs = cosp.next()
            sn = sinp.next()
            P.dma('sp', cs.v, ccos[:, tok0:tok0 + 512])
            P.dma('sp', sn.v, csin[:, tok0:tok0 + 512])
            kt_ = ktk.next()
            for ct in range(8):
                pt = s.psum()
                for k in range(8):
                    P.matmul(pt.v, W[:, k, ct * 128:(ct + 1) * 128], uT[:, k, :], start=(k == 0), stop=(k == 7))
                p2 = s.psum()
                for k in range(8):
                    P.matmul(p2.v, Wsw[:, k, ct * 128:(ct + 1) * 128], uT[:, k, :], start=(k == 0), stop=(k == 7))
                a = tA.next()
                b = tB.next()
                P.tt('dve', a.v, pt.v, cs.v, ALU.mult)
                P.tt('dve', b.v, p2.v, sn.v, ALU.mult)
                P.tt('pool', a.v, a.v, b.v, ALU.add)
                st = stg_b.next()
                P.act(st.v, a.v, AF.Identity, scale=(1.0 if ct < 4 else 0.125))
                dst = QT if ct < 4 else KT
                c0 = (ct % 4) * 128
                P.dma('sp', dst[c0:c0 + 128, tok0:tok0 + 512], st.v)
                if ct >= 4:
                    p3_ = s.psum()
                    pv = p3_.v.bitcast(BF16)[:, 0:512].r('p (j x) -> p j x', j=4)
                    for j in range(4):
                        P.transpose(pv[:, j, :], st[:, j * 128:(j + 1) * 128], s.ident.v)
                    P.copy('act', kt_[:, :, c0:c0 + 128], pv)
            P.dma('sp', KTOK.v.r('(n p) c -> p n c', p=128)[:, blk * 4:(blk + 1) * 4, :], kt_.v)
            for ct in range(8):
                pt = s.psum()
                for k in range(8):
                    P.matmul(pt.v, W[:, k, 2560 + ct * 128:2560 + (ct + 1) * 128], uT[:, k, :], start=(k == 0), stop=(k == 7))
                st = stg_f.next()
                P.copy('act' if ct % 2 else 'dve', st.v, pt.v)
                P.dma('sp', XB[ct * 128:(ct + 1) * 128, tok0:tok0 + 512], st.v)
            for j in range(4):
                lhs = lambda k: uT[:, k, j * 128:(j + 1) * 128]
                rows = slice(tok0 + j * 128, tok0 + (j + 1) * 128)
                for ci, (c0, dstd, isb) in enumerate(((1024, VT, True), (1536, GT, False), (2048, ZT, False))):
                    pt = s.psum()
                    for k in range(8):
                        P.matmul(pt.v, lhs(k), W[:, k, c0:c0 + 512], start=(k == 0), stop=(k == 7))
                    st = stg_b.next() if isb else stg_f.next()
                    P.copy('act' if ci % 2 else 'dve', st.v, pt.v)
                    P.dma('sp', dstd[rows, :], st.v)
                pt = s.psum()
                for k in range(8):
                    P.matmul(pt[:, 0:16], lhs(k), W[:, k, 3584:3600], start=(k == 0), stop=(k == 7))
                st = stg_d.next()
                P.copy('dve', st.v, pt[:, 0:16])
                P.dma('sp', DT[rows, :], st.v)
        P.barrier()
        A.reset(m0)

    def dattn(s):
        cfg, P, A = s.cfg, s.P, s.A
        m0 = A.mark()
        NT, ntile = cfg.ntok, cfg.ntile
        W = s.W
        QT, KT, VT = s.dscr['QT'], s.dscr['KT'], s.dscr['Vtok']
        KTOK, GT, ZT, DT, XB = s.dscr['Ktok'], s.dscr['Gtok'], s.dscr['Ztok'], s.dscr['DTtok'], s.dscr['XBCT']
        YT, YF2 = s.dscr['YT'], s.dscr['YF2']
        ctri = s.inp('c_tri', [2, 128, 128])
        cneg = s.inp('c_negmask', [2, 128, 128])
        tri = A.alloc([2, 128], F32, 'tri')
        neg = A.alloc([2, 128], F32, 'neg')
        for d in range(2):
            P.dma('sp', tri[:, d, :], ctri[d])
            P.dma('sp', neg[:, d, :], cneg[d])
        lgr = A.alloc([16], F32, 'lgr')
        P.dma('sp', lgr.v, V(W['ret_decay_exp'], W['ret_decay_exp'].ap[0].rearrange('a b -> (a b)').partition_broadcast(128)))
        P.act(lgr.v, lgr.v, AF.Exp, scale=-math.log(2.0))
        P.act(lgr.v, lgr.v, AF.Ln, scale=-1.0, bias=s.epsc[:, 3:4])
        gng = s.load_bcast(W['ret_gn_g'][0], 512, 'gng')
        gnb = s.load_bcast(W['ret_gn_b'][0], 512, 'gnb')
        dtb = A.alloc([16], F32, 'dtb')
        P.dma('sp', dtb.v, V(W['ssd_dt_bias'], W['ssd_dt_bias'].ap[0].rearrange('a b -> (a b)').partition_broadcast(128)))
        aneg = A.alloc([16], F32, 'aneg')
        P.dma('sp', aneg.v, V(W['ssd_a_log'], W['ssd_a_log'].ap[0].rearrange('a b -> (a b)').partition_broadcast(128)))
        P.act(aneg.v, aneg.v, AF.Exp)
        P.ts('dve', aneg.v, aneg.v, -1.0, None, ALU.mult)
        dsk = s.load_bcast(W['ssd_d'][0], 8, 'dsk')
        nrg = s.load_bcast(W['ssd_norm_g'][0], 512, 'nrg')
        cw = A.alloc([8, 5], F32, 'cw')
        for j in range(5):
            P.dma('sp', cw[:, :, j], V(W['ssd_conv_w'], W['ssd_conv_w'].ap[0][j].rearrange('(c p) -> p c', p=128)), allow_slow_non_contiguous=True)
        cb = A.alloc([8], F32, 'cb')
        P.dma('sp', cb.v, V(W['ssd_conv_b'], W['ssd_conv_b'].ap[0].rearrange('(c p) -> p c', p=128)), allow_slow_non_contiguous=True)
        def chk(st):
            if cfg.stop_after == st:
                raise StopIteration
        chk('d1')
        qTt = Pool(A, 2, [4, 128], BF16, 'qTt')
        kTt = Pool(A, 2, [4, 128], BF16, 'kTt')
        ktok = Pool(A, 2, [512], BF16, 'ktok')
        vtok = Pool(A, 2, [8, 64], BF16, 'vtok')
        xh = Pool(A, 2, [8, 132], F32, 'xh')
        cacc = Pool(A, 2, [8, 128], F32, 'cacc')
        ctmp = Pool(A, 2, [8, 128], F32, 'ctmp')
        xact = Pool(A, 2, [8, 128], BF16, 'xact')
        xstok = Pool(A, 3, [8, 64], BF16, 'xstok')
        bmtok = Pool(A, 2, [256], BF16, 'bmtok')
        dtt = Pool(A, 2, [16], F32, 'dtt')
        las = Pool(A, 2, [16], F32, 'las')
        vdir = Pool(A, 2, [8, 64], BF16, 'vdir')
        labc = Pool(A, 2, [8, 128], F32, 'labc')
        lapr = Pool(A, 2, [8, 64], F32, 'lapr')
        cumt = Pool(A, 2, [8], F32, 'cumt')
        dmt = Pool(A, 2, [8, 128], F32, 'dmt')
        crw = Pool(A, 2, [4, 128], F32, 'crw')
        PTp = Pool(A, 2, [8, 128], BF16, 'PTp')
        ecp = Pool(A, 2, [4, 128], F32, 'ecp')
        ecf = Pool(A, 2, [8, 128], F32, 'ecf')
        qtl = Pool(A, 2, [8, 128], BF16, 'qtl')
        wend = Pool(A, 2, [8], F32, 'wend')
        wv = Pool(A, 2, [8, 64], BF16, 'wv')
        Hr = A.alloc([4, 64], F32, 'Hr')
        Hrb = A.alloc([4, 64], BF16, 'Hrb')
        Hs = A.alloc([8, 64], F32, 'Hs')
        Hsb = A.alloc([8, 64], BF16, 'Hsb')
        yfp = Pool(A, 2, [1024], F32, 'yf2')
        ysp = Pool(A, 2, [1024], F32, 'ys2')
        sqp = Pool(A, 2, [512], F32, 'sq2')
        gz = Pool(A, 2, [1024], F32, 'gz')
        st8 = Pool(A, 4, [8], F32, 'st8b')
        ybf = Pool(A, 2, [1024], BF16, 'ybf')
        fout = Pool(A, 2, [8, 128], BF16, 'fout')

        def gen(m, d, la, qT_, kT_, ktok_, v_, H, Hb, R):
            endc = 127 if d == 0 else 0
            tri_d = tri[:, d, :]
            off = 0 if m == 'ret' else 512
            pc = yield from s.psum_g()
            P.matmul(pc[:, 0:8], tri_d, la)
            yield
            cum = cumt.next()
            P.copy('dve', cum.v, pc[:, 0:8])
            s.psrel(pc)
            lb = labc.next()
            P.copy('pool', lb.v, V(la.t, la.ap.unsqueeze(2)).bc([128, 8, 128]))
            dm = dmt.next()
            ef = ecf.next() if m == 'ssd' else None
            we = wend.next()
            yield
            for half in range(2):
                pr_ = yield from s.psum_g()
                prv = pr_.v.r('p (h l) -> p h l', h=4)
                for hq in range(4):
                    h = half * 4 + hq
                    P.matmul(prv[:, hq, :], lb[:, h, :], tri_d)
                yield
                hs_ = slice(half * 4, half * 4 + 4)
                if m == 'ssd':
                    cr = crw.next()
                    P.copy('dve', cr.v, prv)
                    s.psrel(pr_)
                    srcv = cr.v
                else:
                    srcv = prv
                P.tt('dve', dm[:, hs_, :], srcv, V(cum.t, cum.ap[:, hs_].unsqueeze(2)).bc([128, 4, 128]), ALU.subtract)
                if m == 'ssd':
                    P.act(ef[:, hs_, :], srcv, AF.Exp)
                P.tt('dve', we[:, hs_], srcv[:, :, endc], cum[:, hs_], ALU.subtract)
                if m != 'ssd':
                    s.psrel(pr_)
                yield
            ng = neg[:, d, :]
            P.tt('pool', dm.v, dm.v, V(ng.t, ng.ap.unsqueeze(1)).bc([128, 8, 128]), ALU.add)
            yield
            P.act(dm.v, dm.v, AF.Exp)
            P.act(we.v, we.v, AF.Exp)
            yield
            PT = PTp.next()
            ql = qtl.next()
            if m == 'ret':
                lp = lapr.next()
                P.copy('pool', lp.v, V(la.t, la.ap.unsqueeze(2)).bc([128, 8, 64]))
                yield
                pp_ = yield from s.psum_g()
                ppv = pp_.v.r('p (c l) -> p c l', c=4)
                lpv = lp.v.r('p (c hh) i -> p c (hh i)', hh=2)
                for c in range(4):
                    P.matmul(ppv[:, c, :], lpv[:, c, :], tri_d)
                yield
                ep = ecp.next()
                P.act(ep.v, ppv, AF.Exp)
                s.psrel(pp_)
                yield
                P.tt('dve', ql[:, 0:4, :], qT_.v, ep.v, ALU.mult)
                tot = ep[:, :, endc]
                for half in range(2):
                    ps_ = yield from s.psum_g()
                    psv = ps_.v.r('p (h l) -> p h l', h=4)
                    for hh in range(2):
                        for hq in range(4):
                            h = half * 4 + hq
                            if h % 2 != hh:
                                continue
                            c = h // 2
                            pr = slice(64 * hh, 64 * hh + 64)
                            P.matmul(psv[:, hq, :], kT_[pr, c, :], qT_[pr, c, :])
                    yield
                    hs_ = slice(half * 4, half * 4 + 4)
                    P.tt('dve', PT[:, hs_, :], psv, dm[:, hs_, :], ALU.mult)
                    s.psrel(ps_)
                    yield
                yb_ = yield from s.psum_g()
                ypsum = yb_.v.r('p (h e) -> p h e', h=8)
                for h in range(8):
                    c, hh = h // 2, h % 2
                    pr = slice(64 * hh, 64 * hh + 64)
                    P.matmul(ypsum[:, h, :], PT[:, h, :], v_[:, h, :], start=True, stop=False)
                    P.matmul(ypsum[:, h, :], ql[pr, c, :], Hb[pr, c, :], start=False, stop=True)
            else:
                P.tt('dve', ql.v.r('p (g q) l -> p g q l', g=2), V(qT_.t, qT_.ap.unsqueeze(2)).bc([128, 2, 4, 128]),
                     ef.v.r('p (g q) l -> p g q l', g=2), ALU.mult)
                tot = ef[:, :, endc]
                yield
                for g in range(2):
                    ps_ = yield from s.psum_g()
                    psv = ps_.v.r('p (h l) -> p h l', h=4)
                    for hq in range(4):
                        P.matmul(psv[:, hq, :], kT_[:, g, :], qT_[:, g, :])
                    yield
                    hs_ = slice(g * 4, g * 4 + 4)
                    P.tt('dve', PT[:, hs_, :], psv, dm[:, hs_, :], ALU.mult)
                    s.psrel(ps_)
                    yield
                yb_ = yield from s.psum_g()
                ypsum = yb_.v.r('p (h e) -> p h e', h=8)
                for h in range(8):
                    P.matmul(ypsum[:, h, :], PT[:, h, :], v_[:, h, :], start=True, stop=False)
                    P.matmul(ypsum[:, h, :], ql[:, h, :], Hb[:, h, :], start=False, stop=True)
            yield
            if d == 0:
                P.copy('act', R['yf'][:, off:off + 512], yb_.v)
            else:
                P.tt('dve', R['ys'][:, off:off + 512], R['yf'][:, off:off + 512], yb_.v, ALU.add)
            s.psrel(yb_)
            yield
            w_ = wv.next()
            P.tt('dve', w_.v, v_.v, V(we.t, we.ap.unsqueeze(2)).bc([128, 8, 64]), ALU.mult)
            yield
            ph = yield from s.psum_g()
            phv = ph.v.r('p (h e) -> p h e', h=8)
            for h in range(8):
                if m == 'ret':
                    c = h // 2
                    P.matmul(phv[:, h, :], ktok_[:, c * 128:(c + 1) * 128], w_[:, h, :])
                else:
                    g = h // 4
                    P.matmul(phv[:, h, :], ktok_[:, g * 128:(g + 1) * 128], w_[:, h, :])
            yield
            if m == 'ret':
                P.tt('dve', H.v, H.v, V(tot.t, tot.ap.unsqueeze(2)).bc([128, 4, 64]), ALU.mult)
                phv2 = ph.v.r('p (c hh e) -> p c hh e', c=4, hh=2)
                for hh in range(2):
                    pr = slice(64 * hh, 64 * hh + 64)
                    P.tt('dve', H[pr, :, :], H[pr, :, :], phv2[pr, :, hh, :], ALU.add)
            else:
                P.tt('dve', H.v, H.v, V(tot.t, tot.ap.unsqueeze(2)).bc([128, 8, 64]), ALU.mult)
                P.tt('dve', H.v, H.v, phv, ALU.add)
            s.psrel(ph)

        def prep(n, d, R):
            ctok = n * 128
            seg = ctok // cfg.seglen
            rows = slice(ctok, ctok + 128)
            R['rows'] = rows
            R['ctok'] = ctok
            qT_ = qTt.next()
            kT_ = kTt.next()
            kk_ = ktok.next()
            vv_ = vtok.next()
            P.dma('sp', qT_.v, QT.v.r('(c p) t -> p c t', p=128)[:, :, rows])
            P.dma('sp', kT_.v, KT.v.r('(c p) t -> p c t', p=128)[:, :, rows])
            P.dma('sp', kk_.v, KTOK[rows, :])
            P.dma('sp', vv_.v.r('p h e -> p (h e)'), VT[rows, :])
            R.update(qT=qT_, kT=kT_, kk=kk_, vv=vv_)
            yf = yfp.next()
            R['yf'] = yf
            if d == 1:
                P.dma('sp', yf.v, YF2[rows, :])
                R['ys'] = ysp.next()
            x_ = xh.next()
            lo, hi = ctok - 2, ctok + 130
            XBv = XB.v.r('(c p) t -> p c t', p=128)
            if lo < 0:
                P.memset('dve', x_[:, :, 0:2], 0.0)
                P.dma('sp', x_[:, :, 2:132], XBv[:, :, 0:hi])
            elif hi > NT:
                P.memset('dve', x_[:, :, 130:132], 0.0)
                P.dma('sp', x_[:, :, 0:130], XBv[:, :, lo:NT])
            else:
                P.dma('sp', x_.v, XBv[:, :, lo:hi])
            if ctok % cfg.seglen == 0 and seg > 0:
                P.ts('dve', x_[:, :, 0:2], x_[:, :, 0:2], s.flags[:, seg:seg + 1], None, ALU.mult)
            if (ctok + 128) % cfg.seglen == 0 and seg + 1 < cfg.nseg:
                P.ts('dve', x_[:, :, 130:132], x_[:, :, 130:132], s.flags[:, seg + 1:seg + 2], None, ALU.mult)
            yield
            acc = cacc.next()
            P.tt('dve', acc.v, x_[:, :, 0:128], cw[:, :, 0:1].bc([128, 8, 128]), ALU.mult)
            for j in range(1, 5):
                tm = ctmp.next()
                P.tt('pool', tm.v, x_[:, :, j:j + 128], cw[:, :, j:j + 1].bc([128, 8, 128]), ALU.mult)
                P.tt('dve', acc.v, acc.v, tm.v, ALU.add)
                yield
            P.tt('dve', acc.v, acc.v, V(cb, cb.ap.unsqueeze(2)).bc([128, 8, 128]), ALU.add)
            xa = xact.next()
            P.act(xa.v, acc.v, AF.Silu)
            yield
            xs_ = xstok.next()
            pt = yield from s.psum_g()
            pv = pt.v.bitcast(BF16)[:, 0:512].r('p (c x) -> p c x', c=4)
            for c in range(4):
                P.transpose(pv[:, c, :], xa[:, c, :], s.ident.v)
            yield
            P.copy('act', xs_.v.r('p h e -> p (h e)'), pt.v.bitcast(BF16)[:, 0:512])
            s.psrel(pt)
            bm_ = bmtok.next()
            pt = yield from s.psum_g()
            pv = pt.v.bitcast(BF16)[:, 0:256].r('p (c x) -> p c x', c=2)
            for c in range(2):
                P.transpose(pv[:, c, :], xa[:, 4 + c, :], s.ident.v)
            yield
            P.copy('act', bm_.v, pt.v.bitcast(BF16)[:, 0:256])
            s.psrel(pt)
            dt_ = dtt.next()
            P.dma('sp', dt_.v, DT[rows, :])
            P.tt('dve', dt_.v, dt_.v, dtb.v, ALU.add)
            yield
            P.act(dt_.v, dt_.v, AF.Exp)
            P.act(dt_.v, dt_.v, AF.Ln, bias=s.epsc[:, 3:4])
            yield
            la_ = las.next()
            P.tt('dve', la_.v, dt_.v, aneg.v, ALU.mult)
            vd = vdir.next()
            P.tt('dve', vd.v, xs_.v, V(dt_.t, dt_.ap[:, d * 8:(d + 1) * 8].unsqueeze(2)).bc([128, 8, 64]), ALU.mult)
            R.update(xa=xa, xs=xs_, bm=bm_, la=la_, vd=vd)

        def fin(R):
            rows = R['rows']
            ys, xs_ = R['ys'], R['xs']
            g_ = gz.next()
            P.dma('sp', g_[:, 0:512], GT[rows, :])
            P.dma('sp', g_[:, 512:1024], ZT[rows, :])
            P.act(g_.v, g_.v, AF.Silu)
            yield
            yr3 = ys[:, 0:512].r('p (h e) -> p h e', h=8)
            s8 = st8.next()
            P.reduce('dve', s8.v, yr3, ALU.add)
            P.ts('dve', s8.v, s8.v, 1.0 / 64, None, ALU.mult)
            P.tt('dve', yr3, yr3, V(s8, s8.ap.unsqueeze(2)).bc([128, 8, 64]), ALU.subtract)
            yield
            sq = sqp.next()
            P.tt('pool', sq.v, ys[:, 0:512], ys[:, 0:512], ALU.mult)
            yield
            v8 = st8.next()
            P.reduce('dve', v8.v, sq.v.r('p (h e) -> p h e', h=8), ALU.add)
            P.act(v8.v, v8.v, AF.Sqrt, scale=1.0 / 64, bias=s.epsc[:, 1:2])
            yield
            P.recip(v8.v, v8.v)
            P.tt('dve', yr3, yr3, V(v8, v8.ap.unsqueeze(2)).bc([128, 8, 64]), ALU.mult)
            yield
            P.tt('pool', ys[:, 0:512], ys[:, 0:512], gng.v, ALU.mult)
            P.tt('pool', ys[:, 0:512], ys[:, 0:512], gnb.v, ALU.add)
            yield
            yb = ybf.next()
            P.tt('dve', yb[:, 0:512], ys[:, 0:512], g_[:, 0:512], ALU.mult)
            sq = sqp.next()
            P.tt('dve', sq.v.r('p (h e) -> p h e', h=8), xs_.v, V(dsk, dsk.ap.unsqueeze(2)).bc([128, 8, 64]), ALU.mult)
            P.tt('dve', ys[:, 512:1024], ys[:, 512:1024], sq.v, ALU.add)
            yield
            P.tt('dve', ys[:, 512:1024], ys[:, 512:1024], g_[:, 512:1024], ALU.mult)
            sq = sqp.next()
            P.tt('pool', sq.v, ys[:, 512:1024], ys[:, 512:1024], ALU.mult)
            yield
            m8 = st8.next()
            P.reduce('dve', m8[:, 0:2], sq.v.r('p (g e) -> p g e', g=2), ALU.add)
            P.act(m8[:, 0:2], m8[:, 0:2], AF.Sqrt, scale=1.0 / 256, bias=s.epsc[:, 0:1])
            yield
            P.recip(m8[:, 0:2], m8[:, 0:2])
            yd2 = ys[:, 512:1024].r('p (g e) -> p g e', g=2)
            P.tt('dve', yd2, yd2, V(m8, m8.ap[:, 0:2].unsqueeze(2)).bc([128, 2, 256]), ALU.mult)
            P.tt('dve', yb[:, 512:1024], ys[:, 512:1024], nrg.v, ALU.mult)
            yield
            fo = fout.next()
            for half in range(2):
                pt = yield from s.psum_g()
                pv = pt.v.bitcast(BF16)[:, 0:512].r('p (c x) -> p c x', c=4)
                for c in range(4):
                    cc = half * 4 + c
                    P.transpose(pv[:, c, :], yb[:, cc * 128:(cc + 1) * 128], s.ident.v)
                yield
                P.copy('act', fo[:, half * 4:(half + 1) * 4, :], pv)
                s.psrel(pt)
            P.dma('sp', YT.v.r('(c p) t -> p c t', p=128)[:, :, rows], fo.v)

        def _drain(g):
            for _ in g:
                pass

        s.ps_free = [s.ps[i] for i in (0, 1, 2, 3, 4, 6, 7)]
        for d in range(2):
            for H_ in (Hr, Hrb, Hs, Hsb):
                P.memset('dve', H_.v, 0.0)
            order = list(range(ntile)) if d == 0 else list(range(ntile))[::-1]
            Rn_ = {}
            _drain(prep(order[0], d, Rn_))
            prevR = None
            for i, n in enumerate(order):
                R = Rn_
                ctok = R['ctok']
                cmT = V(R['xa'], R['xa'].ap[:, 6:8, :])
                bmT = V(R['xa'], R['xa'].ap[:, 4:6, :])
                th = [gen('ret', d, lgr[:, d * 8:(d + 1) * 8], R['qT'], R['kT'], R['kk'], R['vv'], Hr, Hrb, R),
                      gen('ssd', d, R['la'][:, d * 8:(d + 1) * 8], cmT, bmT, R['bm'], R['vd'], Hs, Hsb, R)]
                if i + 1 < len(order):
                    Rn_ = {}
                    th.append(prep(order[i + 1], d, Rn_))
                if d == 1 and prevR is not None:
                    th.append(fin(prevR))
                if getattr(cfg, 'no_il', False):
                    for g_ in th:
                        _drain(g_)
                else:
                    run_threads(th)
                nxt_tok = ctok + 128 if d == 0 else ctok - 128
                if 0 <= nxt_tok < NT and nxt_tok // cfg.seglen != ctok // cfg.seglen:
                    bnd = max(nxt_tok, ctok) // cfg.seglen
                    P.ts('dve', Hr.v, Hr.v, s.flags[:, bnd:bnd + 1], None, ALU.mult)
                    P.ts('dve', Hs.v, Hs.v, s.flags[:, bnd:bnd + 1], None, ALU.mult)
                P.copy('pool', Hrb.v, Hr.v)
                P.copy('pool', Hsb.v, Hs.v)
                if d == 0:
                    P.dma('sp', YF2[R['rows'], :], R['yf'].v)
                prevR = R
            if d == 1:
                _drain(fin(prevR))
            P.barrier()
        A.reset(m0)

    def p3(s, layer, src, dst, out_w):
        cfg, P, A = s.cfg, s.P, s.A
        m0 = A.mark()
        YT = s.dscr['YT']
        Wo = s.load_w_bf16(out_w, 8, DM, 'w_out')
        W1 = s.load_w_bf16(V(s.W['mlp_w1'], s.W['mlp_w1'].ap[layer]), 8, DFF, 'w1')
        W2 = s.load_w_bf16(V(s.W['mlp_w2'], s.W['mlp_w2'].ap[layer]), 32, DM, 'w2')
        gbc = s.load_bcast(s.W['norm_mlp'][layer], DM, 'gbc2')
        s.xpool = Pool(A, 2, [DM], F32, 'x')
        hpool = Pool(A, 2, [DM], F32, 'h')
        s.junk = Pool(A, 1, [DM], BF16, 'junk')
        s.small = Pool(A, 4, [4], F32, 'small')
        s.upool = Pool(A, 2, [DM], BF16, 'u')
        uTp = Pool(A, 1, [8, 256], BF16, 'uT')
        yTp = Pool(A, 2, [8, 256], BF16, 'yT')
        hid = A.alloc([32, 256], BF16, 'hid')
        rl = Pool(A, 2, [256], F32, 'rl')
        for blk in range(cfg.ntok // 256):
            tok0 = blk * 256
            yT = yTp.next()
            P.dma('sp', yT.v, YT.v.r('(k p) t -> p k t', p=128)[:, :, tok0:tok0 + 256])
            uT = uTp.next()
            hts = []
            for j in range(2):
                xt = s.xpool.next()
                P.dma('sp', xt.v, src[tok0 + j * 128: tok0 + (j + 1) * 128, :])
                ht = hpool.next()
                hts.append(ht)
                for half in range(2):
                    pt = s.psum()
                    for k in range(8):
                        P.matmul(pt.v, yT[:, k, j * 128:(j + 1) * 128], Wo[:, k, half * 512:(half + 1) * 512],
                                 start=(k == 0), stop=(k == 7))
                    P.tt('dve', ht[:, half * 512:(half + 1) * 512], xt[:, half * 512:(half + 1) * 512], pt.v, ALU.add)
                s.norm_transpose_tile(ht, j, gbc, uT, s.ident)
            for f in range(32):
                pt = s.psum()
                pv = pt[:, 0:256]
                for k in range(8):
                    P.matmul(pv, W1[:, k, f * 128:(f + 1) * 128], uT[:, k, :], start=(k == 0), stop=(k == 7))
                r = rl.next()
                P.act(r.v, pv, AF.Relu)
                P.tt('pool', hid[:, f, :], r.v, r.v, ALU.mult)
            for j in range(2):
                ht = hts[j]
                for half in range(2):
                    pt = s.psum()
                    for f in range(32):
                        P.matmul(pt.v, hid[:, f, j * 128:(j + 1) * 128], W2[:, f, half * 512:(half + 1) * 512],
                                 start=(f == 0), stop=(f == 31))
                    P.tt('dve', ht[:, half * 512:(half + 1) * 512], ht[:, half * 512:(half + 1) * 512], pt.v, ALU.add)
                P.dma('sp', dst[tok0 + j * 128: tok0 + (j + 1) * 128, :], ht.v)
        P.barrier()
        A.reset(m0)

    def build(s):
        cfg, P, A = s.cfg, s.P, s.A
        NT = cfg.ntok
        x = s.inp('x', [NT, DM])
        s.W = {}
        wshapes = dict(norm_mix=[2, DM], norm_mlp=[2, DM], mlp_w1=[2, DM, DFF], mlp_w2=[2, DFF, DM],
                       even_in_w=[DM, EVEN_IN], even_out_w=[DM, DM], attn_q_gain=[1, 64], attn_k_gain=[1, 64],
                       rwkv_mu_prev=[1, 1920], rwkv_mu_next=[1, 1920], rwkv_w0=[1, 2, 512], rwkv_w2=[1, 2, 64, 512],
                       rwkv_a0=[1, 2, 512], rwkv_a2=[1, 2, 64, 512], rwkv_g2=[1, 128, 512], rwkv_k_k=[1, 512],
                       rwkv_k_a=[1, 512], rwkv_r_k=[1, 8, 64], rwkv_ln_g=[1, 512], rwkv_ln_b=[1, 512],
                       odd_in_w=[DM, ODD_IN], odd_in_w_swap=[DM, 1024], odd_out_w=[DM, DM], ret_decay_exp=[1, 2, 8],
                       ret_gn_g=[1, 512], ret_gn_b=[1, 512], ssd_conv_w=[1, 5, 1024], ssd_conv_b=[1, 1024],
                       ssd_dt_bias=[1, 2, 8], ssd_a_log=[1, 2, 8], ssd_d=[1, 8], ssd_norm_g=[1, 512])
        for k, shp in wshapes.items():
            s.W[k] = s.inp(k, shp)
        s.scratch('QT', [512, NT], BF16)
        s.scratch('KT', [512, NT], BF16)
        s.scratch('Vtok', [NT, 512], BF16)
        s.scratch('RW', [1920, NT], F32)
        s.scratch('YT', [1024, NT], BF16)
        s.scratch('H1', [NT, DM], F32)
        s.scratch('YF', [NT, 512], F32)
        s.scratch('Ktok', [NT, 512], BF16)
        s.scratch('Gtok', [NT, 512], F32)
        s.scratch('Ztok', [NT, 512], F32)
        s.scratch('DTtok', [NT, 16], F32)
        s.scratch('XBCT', [1024, NT], F32)
        s.scratch('YF2', [NT, 1024], F32)
        s.inp('c_cos', [128, NT])
        s.inp('c_sin', [128, NT])
        if cfg.debug:
            s.scratch('YS', [NT, 512], F32)
            s.scratch('YN', [NT, 512], BF16)
        yout = s.out('y_out', [NT, DM])
        s.common_consts()
        fl = s.inp('flags', [128, 8])
        s.flags = A.alloc([8], F32, 'flags')
        P.dma('sp', s.flags.v, fl.v)
        s.p1_even(x.v)
        if cfg.stop_after == 'p1even':
            return s.finish()
        s.attention()
        if cfg.stop_after == 'attn':
            return s.finish()
        if 'rwkv' not in cfg.skip:
            s.rwkv()
        else:
            z = A.alloc([512], BF16, 'z')
            P.memset('dve', z.v, 0.0)
            for c in range(4, 8):
                for t0 in range(0, NT, 512):
                    P.dma('sp', s.dscr['YT'][c * 128:(c + 1) * 128, t0:t0 + 512], z.v)
        if cfg.stop_after == 'rwkv':
            return s.finish()
        s.p3(0, x.v, s.dscr['H1'].v, V(s.W['even_out_w'], s.W['even_out_w'].ap))
        if cfg.stop_after == 'l0':
            return s.finish()
        s.p1_odd(s.dscr['H1'].v)
        if cfg.stop_after == 'p1odd':
            return s.finish()
        try:
            s.dattn()
        except StopIteration:
            return s.finish()
        s.p3(1, s.dscr['H1'].v, yout.v, V(s.W['odd_out_w'], s.W['odd_out_w'].ap))
        return s.finish()

    def finish(s):
        s.P.barrier()
        s.P.emit()
        return s.nc
from concourse.bass_utils import run_bass_kernel_spmd


_CACHE = {}


def _assign():
    plan = []
    for c in range(8):
        if c < 2:
            segs = [('p', c, i) for i in range(4)] + [('s', c, 0)]
            fl = [0.0, 1.0, 1.0, 1.0, 0.0, 0.0, 0.0, 0.0]
        else:
            segs = [('s', 2 + 5 * (c - 2) + i, 0) for i in range(5)]
            fl = [0.0] * 8
        plan.append((segs, fl))
    return plan


def kernel(**inputs):
    cfg = Cfg(nseg=5, seglen=2048, ncores=8)
    if 'k' not in _CACHE:
        kb = K(cfg)
        kb.build()
        _CACHE['k'] = kb
    kb = _CACHE['k']
    nc = kb.nc
    xp = np.asarray(inputs['x_prompt'], dtype=np.float32)
    xs = np.asarray(inputs['x_sample'], dtype=np.float32)
    hc = host_consts(cfg)
    plan = _assign()
    w_sw = swap_cols(np.asarray(inputs['odd_in_w'], dtype=np.float32)[0])
    in_maps = []
    for c in range(8):
        segs, fl = plan[c]
        rows = []
        for kind, b, i in segs:
            rows.append(xp[b, i * 2048:(i + 1) * 2048] if kind == 'p' else xs[b])
        pos = np.concatenate([(i * 2048 + np.arange(2048)) if kind == 'p' else np.arange(2048)
                              for kind, b, i in segs])
        cos2, sin2 = rotary_tables(pos)
        m = {'x': np.ascontiguousarray(np.concatenate(rows, 0)), 'c_cos': cos2, 'c_sin': sin2,
             'flags': np.ascontiguousarray(np.broadcast_to(np.asarray(fl, np.float32)[None, :], (128, 8)))}
        for k in kb.W:
            if k == 'odd_in_w_swap':
                m[k] = w_sw
            else:
                m[k] = np.ascontiguousarray(np.asarray(inputs[k], dtype=np.float32)).reshape(kb.din[k].ap.shape)
        for k in kb.din:
            if k.startswith('c_') and k not in m:
                m[k] = hc[k[2:]]
        in_maps.append(m)
    res = run_bass_kernel_spmd(nc, in_maps, core_ids=list(range(8)))
    yp = np.zeros((2, 8192, 1024), np.float32)
    ys = np.zeros((32, 2048, 1024), np.float32)
    for c in range(8):
        o = np.asarray(res.results[c]['y_out'], dtype=np.float32)
        for j, (kind, b, i) in enumerate(plan[c][0]):
            blk = o[j * 2048:(j + 1) * 2048]
            if kind == 'p':
                yp[b, i * 2048:(i + 1) * 2048] = blk
            else:
                ys[b] = blk
    return (yp, ys)
```
